# Optimizing a Trainium2 kernel written in Bass

```python
import math
import jax, jax.numpy as jnp
from jax import lax
import numpy as np

D_MODEL = 1024
BATCH = 4
SEQ = 8192
DEPTH = 1

PLE_DIM = 256
GLA_HEADS = 4
GLA_DK = D_MODEL // 2 // GLA_HEADS
GLA_DV = D_MODEL // GLA_HEADS
GLA_GATE_RANK = 16
GLA_GATE_TEMP = 16.0
GLA_CHUNK = 64
DSA_HEADS = 16
DSA_HEAD_DIM = D_MODEL // DSA_HEADS
DSA_LATENT = 128
IDX_HEADS = 8
IDX_DIM = 64
TOPK_MAX = 256
Q_BLOCK = 128
D_FF = 2816
CONV_W = 3
LN_EPS = 1e-5
DEEPNORM_ALPHA = (2.0 * DEPTH) ** 0.25
DEEPNORM_BETA = (8.0 * DEPTH) ** -0.25
SPLIT_SIZES = (
    GLA_HEADS * GLA_DK,
    GLA_HEADS * GLA_DK,
    GLA_HEADS * GLA_DV,
    GLA_HEADS * GLA_DV,
    GLA_GATE_RANK,
    DSA_HEADS * DSA_LATENT,
    DSA_LATENT,
    IDX_HEADS * IDX_DIM,
    IDX_DIM,
    IDX_HEADS,
    2 * D_MODEL,
)
IN_COLS = sum(SPLIT_SIZES)

kernel_name = 'hybrid_gla_dsa_convffn_block'


def _split_points():
    return [int(v) for v in np.cumsum(np.array(SPLIT_SIZES))[:-1]]


def layer_norm(x, g, b):
    xf = x.astype(jnp.float32)
    mu = jnp.mean(xf, axis=-1, keepdims=True)
    var = jnp.mean(jnp.square(xf - mu), axis=-1, keepdims=True)
    return ((xf - mu) * lax.rsqrt(var + LN_EPS) * g.astype(jnp.float32) + b.astype(jnp.float32)).astype(x.dtype)


def rms_norm(x, g):
    xf = x.astype(jnp.float32)
    ms = jnp.mean(jnp.square(xf), axis=-1, keepdims=True)
    return (xf * lax.rsqrt(ms + LN_EPS) * g.astype(jnp.float32)).astype(x.dtype)


def gla_chunked(q, k, v, log_a):
    q, k, v, log_a = (t.astype(jnp.float32) for t in (q, k, v, log_a))
    B, L, H, dk = q.shape
    dv = v.shape[-1]
    C = GLA_CHUNK
    N = L // C

    def to_chunks(t):
        return t.reshape(B, N, C, H, t.shape[-1]).transpose(1, 0, 3, 2, 4)

    qc, kc, vc, ac = (to_chunks(t) for t in (q, k, v, log_a))
    b = jnp.cumsum(ac, axis=3)
    b_last = b[:, :, :, -1:, :]
    q_in = qc * jnp.exp(b)
    k_in = kc * jnp.exp(-b)
    k_state = kc * jnp.exp(b_last - b)
    causal = jnp.tril(jnp.ones((C, C), dtype=bool))
    attn = jnp.where(causal, jnp.einsum('nbhid,nbhjd->nbhij', q_in, k_in), 0.0)
    o_intra = jnp.einsum('nbhij,nbhjv->nbhiv', attn, vc)

    def step(S, xs):
        q_i, k_i, v_i, decay_i = xs
        o = jnp.einsum('bhid,bhdv->bhiv', q_i, S)
        S = S * decay_i[:, :, 0, :, None] + jnp.einsum('bhjd,bhjv->bhdv', k_i, v_i)
        return S, o

    S0 = jnp.zeros((B, H, dk, dv), jnp.float32)
    _, o_inter = lax.scan(step, S0, (q_in, k_state, vc, jnp.exp(b_last)))
    o = o_intra + o_inter
    return o.transpose(1, 0, 3, 2, 4).reshape(B, L, H, dv)


def dsa_attention(q, ckv, iq, ik, iw):
    B, L, H, dc = q.shape
    top_k = min(TOPK_MAX, L // 4)
    nb = L // Q_BLOCK
    key_pos = jnp.arange(L, dtype=jnp.int32)

    def blockify(t):
        return jnp.swapaxes(t.reshape((B, nb, Q_BLOCK) + t.shape[2:]), 0, 1)

    def one_block(args):
        q_b, iq_b, iw_b, q_pos = args
        s = jax.nn.relu(jnp.einsum('bqhd,bsd->bqhs', iq_b, ik).astype(jnp.float32))
        score = jnp.einsum('bqh,bqhs->bqs', iw_b.astype(jnp.float32), s)
        causal = key_pos[None, :] <= q_pos[:, None]
        score = jnp.where(causal[None], score, -jnp.inf)
        _, idx = lax.top_k(score, top_k)
        valid = idx <= q_pos[None, :, None]
        kv_sel = jax.vmap(lambda c, i: c[i])(ckv, idx)
        logits = jnp.einsum('bqhc,bqkc->bqhk', q_b, kv_sel).astype(jnp.float32) * (dc ** -0.5)
        logits = jnp.where(valid[:, :, None, :], logits, -jnp.inf)
        probs = jax.nn.softmax(logits, axis=-1).astype(kv_sel.dtype)
        return jnp.einsum('bqhk,bqkc->bqhc', probs, kv_sel)

    pos_blocks = jnp.arange(L, dtype=jnp.int32).reshape(nb, Q_BLOCK)
    out = lax.map(one_block, (blockify(q), blockify(iq), blockify(iw), pos_blocks))
    return jnp.swapaxes(out, 0, 1).reshape(B, L, H, dc)


def causal_dwconv(h, w, b):
    L = h.shape[1]
    hp = jnp.pad(h, ((0, 0), (CONV_W - 1, 0), (0, 0)))
    out = w[CONV_W - 1] * h
    for j in range(CONV_W - 1):
        out = out + w[j] * hp[:, j:j + L]
    return out + b


def setup_inputs(seed: int = 0) -> dict:
    key = jax.random.key(seed)
    ks = jax.random.split(key, 24)

    def nrm(k, shape, fan_in, scale=1.0):
        return jax.random.normal(k, shape, jnp.float32) * (scale * fan_in ** -0.5)

    def gain(k, shape):
        return 1.0 + 0.02 * jax.random.normal(k, shape, jnp.float32)

    def bias(k, shape, s=0.02):
        return s * jax.random.normal(k, shape, jnp.float32)

    return {
        'x': jax.random.normal(ks[0], (BATCH, SEQ, D_MODEL), jnp.float32),
        'p': jax.random.normal(ks[1], (DEPTH, BATCH, SEQ, PLE_DIM), jnp.float32),
        'w_in': nrm(ks[2], (DEPTH, D_MODEL, IN_COLS), D_MODEL),
        'w_gla_gate_up': nrm(ks[3], (DEPTH, GLA_GATE_RANK, GLA_HEADS * GLA_DK), GLA_GATE_RANK),
        'b_gla_gate': bias(ks[4], (DEPTH, GLA_HEADS * GLA_DK), 0.1),
        'g_gla_norm': gain(ks[5], (DEPTH, GLA_HEADS * GLA_DV)),
        'w_gla_proj': nrm(ks[6], (DEPTH, GLA_HEADS * GLA_DV, D_MODEL), GLA_HEADS * GLA_DV),
        'g_ckv_norm': gain(ks[7], (DEPTH, DSA_LATENT)),
        'w_uv': nrm(ks[8], (DEPTH, DSA_HEADS, DSA_LATENT, DSA_HEAD_DIM), DSA_LATENT),
        'w_dsa_proj': nrm(ks[9], (DEPTH, DSA_HEADS * DSA_HEAD_DIM, D_MODEL), DSA_HEADS * DSA_HEAD_DIM),
        'w_out': nrm(ks[10], (DEPTH, D_MODEL, D_MODEL), D_MODEL, DEEPNORM_BETA),
        'ln1_g': gain(ks[11], (DEPTH, D_MODEL)),
        'ln1_b': bias(ks[12], (DEPTH, D_MODEL)),
        'w_up': nrm(ks[13], (DEPTH, D_MODEL, 2 * D_FF), D_MODEL),
        'conv_w': nrm(ks[14], (DEPTH, CONV_W, 2 * D_FF), CONV_W),
        'conv_b': bias(ks[15], (DEPTH, 2 * D_FF)),
        'w_down': nrm(ks[16], (DEPTH, D_FF, D_MODEL), D_FF, DEEPNORM_BETA),
        'ln2_g': gain(ks[17], (DEPTH, D_MODEL)),
        'ln2_b': bias(ks[18], (DEPTH, D_MODEL)),
        'w_ple': nrm(ks[19], (DEPTH, PLE_DIM, D_MODEL), PLE_DIM, DEEPNORM_BETA),
        'w_ple_gate': nrm(ks[20], (DEPTH, D_MODEL, D_MODEL), D_MODEL),
        'ln3_g': gain(ks[21], (DEPTH, D_MODEL)),
        'ln3_b': bias(ks[22], (DEPTH, D_MODEL)),
    }


def reference(x, p, w_in, w_gla_gate_up, b_gla_gate, g_gla_norm, w_gla_proj, g_ckv_norm, w_uv,
              w_dsa_proj, w_out, ln1_g, ln1_b, w_up, conv_w, conv_b, w_down, ln2_g, ln2_b,
              w_ple, w_ple_gate, ln3_g, ln3_b):
    B, L, _ = x.shape
    for i in range(DEPTH):
        proj = x @ w_in[i]
        gq, gk, gv, gr, ga, dq, ckv, iq, ik, iw, gates = jnp.split(proj, _split_points(), axis=-1)

        q_a = gq.reshape(B, L, GLA_HEADS, GLA_DK) * (GLA_DK ** -0.5)
        k_a = gk.reshape(B, L, GLA_HEADS, GLA_DK)
        v_a = gv.reshape(B, L, GLA_HEADS, GLA_DV)
        z = (ga @ w_gla_gate_up[i] + b_gla_gate[i]).astype(jnp.float32)
        log_a = (jax.nn.log_sigmoid(z) / GLA_GATE_TEMP).reshape(B, L, GLA_HEADS, GLA_DK)
        o_a = gla_chunked(q_a, k_a, v_a, log_a)
        o_a = rms_norm(o_a, g_gla_norm[i].reshape(GLA_HEADS, GLA_DV))
        o_a = o_a.reshape(B, L, GLA_HEADS * GLA_DV).astype(x.dtype) * jax.nn.silu(gr)
        y_a = o_a @ w_gla_proj[i]

        q_b = dq.reshape(B, L, DSA_HEADS, DSA_LATENT)
        c_kv = rms_norm(ckv, g_ckv_norm[i])
        iq_b = iq.reshape(B, L, IDX_HEADS, IDX_DIM)
        iw_b = iw * ((IDX_HEADS ** -0.5) * (IDX_DIM ** -0.5))
        o_b = dsa_attention(q_b, c_kv, iq_b, ik, iw_b)
        o_b = jnp.einsum('blhc,hcd->blhd', o_b, w_uv[i]).reshape(B, L, DSA_HEADS * DSA_HEAD_DIM)
        y_b = o_b @ w_dsa_proj[i]

        gate_a, gate_b = jnp.split(gates, 2, axis=-1)
        mixed = (jax.nn.sigmoid(gate_a) * y_a + jax.nn.sigmoid(gate_b) * y_b) @ w_out[i]
        x = layer_norm(DEEPNORM_ALPHA * x + mixed, ln1_g[i], ln1_b[i])

        h = causal_dwconv(x @ w_up[i], conv_w[i], conv_b[i])
        h_gate, h_val = jnp.split(h, 2, axis=-1)
        ffn = (jax.nn.silu(h_gate) * h_val) @ w_down[i]
        x = layer_norm(DEEPNORM_ALPHA * x + ffn, ln2_g[i], ln2_b[i])

        ple = jax.nn.sigmoid(x @ w_ple_gate[i]) * (p[i] @ w_ple[i])
        x = layer_norm(DEEPNORM_ALPHA * x + ple, ln3_g[i], ln3_b[i])
    return x
```

```python
import contextlib
import numpy as np
import concourse.bass as bass
import concourse.mybir as mybir
from concourse.bass_utils import run_bass_kernel_spmd

F32 = mybir.dt.float32
BF16 = mybir.dt.bfloat16
ALU = mybir.AluOpType
AF = mybir.ActivationFunctionType

ENGS = ("pe", "act", "dve", "pool", "sp")


class Buf:
    __slots__ = ("name", "w", "r")

    def __init__(self, name):
        self.name = name
        self.w = None
        self.r = {}


class Sched:
    EPOCH = 12000
    NDMA = 24

    def __init__(self, nc):
        self.nc = nc
        self.ops = {e: [] for e in ENGS}
        self.ccount = {e: 0 for e in ENGS}
        self.waited = {e: {} for e in ENGS}
        self.ndma = 0
        self.out_dma_tokens = []
        self.stack = contextlib.ExitStack()
        self.sems = {}
        self.nbuf = 0

    def sbuf(self, name, shape, dtype):
        return self.stack.enter_context(self.nc.sbuf_tensor(name, list(shape), dtype))

    def psum(self, name, shape, dtype):
        return self.stack.enter_context(self.nc.psum_tensor(name, list(shape), dtype))

    def buf(self, name=None):
        self.nbuf += 1
        return Buf(name or f"b{self.nbuf}")

    def _sem(self, key):
        if key not in self.sems:
            nm = "s_" + "_".join(str(k) for k in key)
            self.sems[key] = self.stack.enter_context(self.nc.semaphore(nm))
        return self.sems[key]

    def _tokwait(self, tok):
        if tok[0] == "e":
            _, eng, idx = tok
            return (eng, idx // self.EPOCH), idx % self.EPOCH + 1
        _, k = tok
        return ("dma", k % self.NDMA), 16 * (k // self.NDMA + 1)

    def _resolve(self, eng, deps):
        need = {}
        for tok in deps:
            key, val = self._tokwait(tok)
            if val > need.get(key, 0):
                need[key] = val
        waits = []
        wd = self.waited[eng]
        for key, val in need.items():
            if wd.get(key, 0) >= val:
                continue
            wd[key] = val
            waits.append((key, val))
        return waits

    def _deps(self, reads, writes):
        deps = set()
        for b in reads:
            if b.w is not None:
                deps.add(b.w)
        for b in writes:
            if b.w is not None:
                deps.add(b.w)
            deps.update(b.r.values())
        return deps

    def _mark(self, tok, rkey, reads, writes):
        for b in reads:
            b.r[rkey] = tok
        for b in writes:
            b.w = tok
            b.r = {}

    def op(self, eng, fn, reads=(), writes=()):
        idx = self.ccount[eng]
        tok = ("e", eng, idx)
        waits = self._resolve(eng, self._deps(reads, writes))
        self.ccount[eng] = idx + 1
        self._mark(tok, eng, reads, writes)
        self.ops[eng].append((fn, waits, ((eng, idx // self.EPOCH), 1)))
        return tok

    def dma(self, eng, fn, reads=(), writes=(), is_output=False):
        k = self.ndma
        self.ndma += 1
        tok = ("d", k)
        deps = self._deps(reads, writes)
        if k >= self.NDMA:
            deps.add(("d", k - self.NDMA))
        waits = self._resolve(eng, deps)
        self._mark(tok, ("d", k), reads, writes)
        self.ops[eng].append((fn, waits, (("dma", k % self.NDMA), 16)))
        if is_output:
            self.out_dma_tokens.append(tok)
        return tok

    def finish(self, eng="sp"):
        waits = self._resolve(eng, set(self.out_dma_tokens))
        self.ops[eng].append((None, waits, None))

    def emit(self):
        nc = self.nc
        for e in ENGS:
            for fn, waits, inc in self.ops[e]:
                for key, _ in waits:
                    self._sem(key)
                if inc is not None:
                    self._sem(inc[0])
        ops = self.ops
        sems = self.sems

        def replay(e, engine):
            for fn, waits, inc in ops[e]:
                for key, val in waits:
                    engine.wait_ge(sems[key], val)
                if fn is None:
                    continue
                ins = fn(engine)
                ins.then_inc(sems[inc[0]], inc[1])

        with nc.Block() as block:
            @block.tensor
            def _(eng):
                replay("pe", eng)

            @block.scalar
            def _(eng):
                replay("act", eng)

            @block.vector
            def _(eng):
                replay("dve", eng)

            @block.gpsimd
            def _(eng):
                replay("pool", eng)

            @block.sync
            def _(eng):
                replay("sp", eng)

    def close(self):
        self.stack.close()


def I(method, *args, **kw):
    return lambda e: getattr(e, method)(*args, **kw)


D = 1024
KD = 8
DFF = 2816
NCC = 44
PLE = 256
BIG = 1.0e30
ALPHA = 2.0 ** 0.25
EPS = 1e-5
C_K, C_V, C_SM, C_Q, C_GR, C_DQ, C_IQ, C_GT = 0, 512, 1536, 1816, 2328, 3352, 5400, 5912
NC2 = 7960
O_ID, O_TRIN, O_UN, O_TRI01, O_DB, O_DBP, O_P2 = 0, 128, 256, 384, 512, 640, 768
O_SEL = 832
NCST = 840
V_GN, V_L1G, V_L1B, V_L2G, V_L2B, V_L3G, V_L3B = range(7)


def build_program(L, G, topk, NIT=16, dbg_names=()):
    NB = L // 128
    NOWN = NB // 2
    NGRP = NB // G
    assert NGRP % 2 == 0 and G % 4 == 0
    nc = bass.Bass("TRN2", target_bir_lowering=False)

    def din(name, shape, dt=F32):
        return nc.dram_tensor(name, list(shape), dt, kind="ExternalInput").ap()

    xfr = din("xfr", [L, D])
    pown = din("pown", [NOWN * 128, PLE])
    win2 = din("win2", [D, NC2])
    wgu = din("wgu", [17, 512])
    wgp = din("wgp", [D, D])
    wdp = din("wdp", [D, D])
    wout = din("wout", [D, D])
    wup2 = din("wup2", [D, 2 * DFF])
    wdn = din("wdn", [DFF, D])
    wpg = din("wpg", [D, D])
    wple = din("wple", [PLE, D])
    wuvp = din("wuvp", [128, 2048])
    vecs = din("vecs", [7, D])
    gckv = din("gckv", [1, 128])
    convp = din("convp", [128, NCC * 4])
    cst = din("cst", [128, NCST])
    kvb = din("kvb", [128, 520])
    hflag = din("hflag", [128, 8])
    yown = nc.dram_tensor("yown", [NOWN * 128, D], F32, kind="ExternalOutput").ap()
    dbg_out = {}
    for nm, shp in dbg_names:
        dbg_out[nm] = nc.dram_tensor("dbg_" + nm, list(shp), F32, kind="ExternalOutput").ap()

    def dscr(name, shape):
        return nc.dram_tensor(name, list(shape), BF16, kind="Internal").ap()

    S = Sched(nc)

    wsrc = {"win": (win2, D, NC2), "wgp": (wgp, D, D), "wdp": (wdp, D, D), "wout": (wout, D, D),
            "wup": (wup2, D, 2 * DFF), "wdn": (wdn, DFF, D), "wpg": (wpg, D, D), "wple": (wple, PLE, D)}
    wb16 = {}
    wrow = {}
    for nm, (src, rows, cols) in wsrc.items():
        dst = dscr("b16_" + nm, [rows, cols])
        wb16[nm] = dst
        nrb = rows // 128
        wrow[nm] = [S.buf(f"{nm}_r{i}") for i in range(nrb)]

    wpiece = {}

    pending_precast = []

    def precast2(nm, defer=False):
        src, rows, cols = wsrc[nm]
        dst = wb16[nm]
        pcs = []
        for i in range(rows // 128):
            c0 = 0
            while c0 < cols:
                c1 = min(cols, c0 + 2048)
                b = S.buf(f"{nm}_{i}_{c0}")

                def f(e, i=i, c0=c0, c1=c1, dst=dst, src=src):
                    return e.dma_start(out=dst[i * 128:(i + 1) * 128, c0:c1], in_=src[i * 128:(i + 1) * 128, c0:c1])
                if defer:
                    pending_precast.append((f, b))
                else:
                    S.dma("pool", f, writes=[b])
                pcs.append((i, c0, c1, b))
                c0 = c1
        wpiece[nm] = pcs

    def flush_precast(n):
        for _ in range(min(n, len(pending_precast))):
            f, b = pending_precast.pop(0)
            S.dma("pool", f, writes=[b])

    def wdeps(nm, kc0, nkc, c0, c1):
        return [b for (i, a0, a1, b) in wpiece[nm] if kc0 <= i < kc0 + nkc and a0 < c1 and c0 < a1]

    def tile(name, cols, dt=F32, parts=128):
        return S.sbuf(name, [parts, cols], dt), S.buf(name)

    LK = NB * 128
    ckvT, b_ckvT = tile("ckvT", LK, BF16)
    ckvS, b_ckvS = tile("ckvS", LK, BF16)
    ikT2, b_ikT2 = tile("ikT2", LK, BF16)
    score, b_score = tile("score", LK, F32)
    vecbc, b_vec = tile("vecbc", 7 * D, F32)
    gckvbc, b_gckv = tile("gckvbc", 128, F32)
    cstt, b_cst = tile("cstt", NCST, F32)
    identb, b_idb = tile("identb", 128, BF16)
    onesb, b_ones = tile("onesb", 128, BF16)
    kvbt, b_kvb = tile("kvbt", 520, F32)
    hflt, b_hfl = tile("hflt", 8, F32)
    convt, b_conv = tile("convt", NCC * 4, F32)
    wgub, b_wgu = tile("wgub", 512, BF16, parts=17)
    wuvb, b_wuv = tile("wuvb", 2048, BF16)
    Sst, b_S = tile("Sst", 1024, F32)
    Sbf, b_Sbf = tile("Sbf", 1024, BF16)
    x1Te, b_x1Te = tile("x1Te", KD * 130, BF16)
    NW = 2
    wst = [tile(f"wst{i}", 4096, BF16) for i in range(NW)]
    xbs = [tile(f"xb{i}", D, BF16) for i in range(1)]
    FA, b_FA = tile("FA", D, F32)
    FB, b_FB = tile("FB", D, F32)
    FC, b_FC = tile("FC", D, F32)
    FD, b_FD = tile("FD", D, F32)
    B0, b_B0 = tile("B0", D, BF16)
    B1, b_B1 = tile("B1", D, BF16)
    B2, b_B2 = tile("B2", D, BF16)
    B3, b_B3 = tile("B3", D, BF16)
    R0, b_R0 = tile("R0", 512, F32)
    R1, b_R1 = tile("R1", 512, F32)
    R2, b_R2 = tile("R2", 512, F32)
    Q0, b_Q0 = tile("Q0", 512, BF16)
    Q1, b_Q1 = tile("Q1", 512, BF16)
    Q2, b_Q2 = tile("Q2", 512, BF16)
    Q3, b_Q3 = tile("Q3", 512, BF16)
    gaT, b_gaT = tile("gaT", 128, BF16, parts=17)
    Z, b_Z = tile("Z", DFF, BF16)
    sgT, b_sgT = tile("sgT", 2048, BF16)
    OTn, b_OTn = tile("OTn", DFF, BF16)
    fa = [tile(f"fa{i}", 128, F32) for i in range(4)]
    pbt, b_pbt = tile("pbt", 256, BF16)
    pTt, b_pTt = tile("pTt", 256, BF16)
    sm, b_sm = tile("sm", 64, F32)
    bis, b_bis = tile("bis", 64, F32)
    hcol, b_hcol = tile("hcol", 32, F32)
    sinkA = S.sbuf("sinkA", [128, 8], BF16)
    sinkD = S.sbuf("sinkD", [128, 8], BF16)
    FE, b_FE = tile("FE", D, F32)
    PT = [tile(f"PT{i}", 512, BF16) for i in range(2)]
    PM = [tile(f"PM{i}", 512, BF16) for i in range(2)]
    MASK_ENG = "dve"
    b_rb = [S.buf(f"rb{i}") for i in range(4)]
    b_pmx = [S.buf(f"pmx{i}") for i in range(2)]
    b_c, b_cntD, b_cntA, b_btmp, b_thr = S.buf("b_c"), S.buf("b_cntD"), S.buf("b_cntA"), S.buf("b_btmp"), S.buf("b_thr")

    PP = [S.psum(f"pp{i}", [128, 1024], F32) for i in range(4)]
    PB = [[S.buf(f"pp{i}_{j}") for j in range(2)] for i in range(4)]
    rr = {"h": 0, "p": 0}

    def ph():
        i = rr["h"] % 8
        rr["h"] += 1
        return PP[i // 2][:, (i % 2) * 512:(i % 2 + 1) * 512], [PB[i // 2][i % 2]]

    def pp():
        i = rr["p"] % 4
        rr["p"] += 1
        return PP[i][:, :], PB[i]

    def dbg(nm, ap, bufs):
        if nm in dbg_out:
            o = dbg_out[nm]
            S.dma("sp", I("dma_start", out=o[:, :], in_=ap), reads=bufs, is_output=True)

    S.dma("sp", I("dma_start", out=cstt[:], in_=cst[:, :]), writes=[b_cst])
    S.dma("sp", I("dma_start", out=kvbt[:], in_=kvb[:, :]), writes=[b_kvb])
    S.dma("sp", I("dma_start", out=hflt[:], in_=hflag[:, :]), writes=[b_hfl])
    S.dma("sp", I("dma_start", out=convt[:], in_=convp[:, :]), writes=[b_conv])
    S.dma("sp", I("dma_start", out=gckvbc[:], in_=gckv[0:1, :].partition_broadcast(128)), writes=[b_gckv])
    for r in range(7):
        S.dma("sp", I("dma_start", out=vecbc[:, r * D:(r + 1) * D], in_=vecs[r:r + 1, :].partition_broadcast(128)),
              writes=[b_vec])
    S.dma("pool", I("dma_start", out=wgub[:], in_=wgu[:, :]), writes=[b_wgu])
    S.dma("pool", I("dma_start", out=wuvb[:], in_=wuvp[:, :]), writes=[b_wuv])
    S.dma("pool", I("dma_start", out=identb[:], in_=cst[:, O_ID:O_ID + 128]), writes=[b_idb])
    for nm in ("win", "wgp", "wdp", "wout", "wup", "wdn", "wpg", "wple"):
        precast2(nm, defer=(nm != "win"))
    S.op("dve", I("memset", onesb[:], 1.0), writes=[b_ones])
    S.op("dve", I("memset", Sst[:], 0.0), writes=[b_S])
    S.op("dve", I("memset", Sbf[:], 0.0), writes=[b_Sbf])
    S.op("dve", I("memset", gaT[:], 1.0), writes=[b_gaT])
    S.op("dve", I("memset", x1Te[:], 0.0), writes=[b_x1Te])

    def C(off, n=128):
        return cstt[:, off:off + n]

    wrot = {"i": 0}

    NAL = min(4, LK // 2048)
    b_al = [S.buf(f"alias{k}") for k in range(NAL)]
    al_slots = [(score[:, k * 2048:(k + 1) * 2048].bitcast(BF16), b_al[k]) for k in range(NAL)]
    al = {"ok": True, "i": 0}

    def alias_open():
        S.op("dve", I("memset", sinkD[:, 4:5], 0.0), reads=[b_score], writes=b_al)
        al["ok"] = True

    def alias_close():
        al["ok"] = False
        S.op("dve", I("memset", sinkD[:, 5:6], 0.0), writes=b_al + [b_score])

    def wstream(nm, kc0, nkc, c0, ncols, base_only=False):
        if al["ok"] and not base_only:
            i = al["i"] % (NW + NAL)
            al["i"] += 1
            t, b = (wst[i][0][:, :], wst[i][1]) if i < NW else al_slots[i - NW]
        else:
            i = wrot["i"] % NW
            wrot["i"] += 1
            t, b = wst[i][0][:, :], wst[i][1]
        src = wb16[nm].rearrange("(kc p) c -> p kc c", p=128)[:, kc0:kc0 + nkc, c0:c0 + ncols]
        dst = t[:, 0:nkc * ncols].rearrange("p (kc c) -> p kc c", kc=nkc)
        S.dma("sp", I("dma_start", out=dst, in_=src), reads=wdeps(nm, kc0, nkc, c0, c0 + ncols), writes=[b])
        return t, b

    def mm_group(items):
        def f(e):
            ins = None
            for (o, l, r, st, sp) in items:
                ins = e.matmul(o, l, r, start=st, stop=sp)
            return ins
        return f

    def transposes8(src, b_src, dst, b_dst, ncols=128, nch=8, eng_copy="act"):
        ps, pb_ = ph()
        psb = ps.bitcast(BF16)

        def f(e):
            ins = None
            for k in range(nch):
                ins = e.transpose(psb[:, k * 128:(k + 1) * 128], src[:, k * 128:(k + 1) * 128], identb[:])
            return ins
        S.op("pe", f, reads=[b_src, b_idb], writes=pb_)
        if eng_copy == "act":
            S.op("act", I("copy", dst, psb[:, 0:nch * 128]), reads=pb_, writes=[b_dst])
        else:
            S.op("dve", I("tensor_copy", dst, psb[:, 0:nch * 128]), reads=pb_, writes=[b_dst])

    def rstd_from_sumsq(col_in, col_out, n, inv_count):
        S.op("act", I("activation", sm[:, col_out:col_out + n], sm[:, col_in:col_in + n], AF.Ln, bias=EPS, scale=inv_count),
             reads=[b_sm], writes=[b_sm])
        S.op("act", I("activation", sm[:, col_out:col_out + n], sm[:, col_out:col_out + n], AF.Exp, scale=-0.5),
             reads=[b_sm], writes=[b_sm])

    def layer_norm(xin, b_in, xout, b_out, gcol, bcol):
        snk = sinkA[:, 0:1].broadcast_to([128, D])
        S.op("act", I("activation", snk, xin, AF.Identity, accum_out=sm[:, 0:1]), reads=[b_in], writes=[b_sm])
        S.op("act", I("activation", snk, xin, AF.Square, accum_out=sm[:, 1:2]), reads=[b_in], writes=[b_sm])
        S.op("dve", I("tensor_scalar", sm[:, 2:3], sm[:, 0:1], 1.0 / D, None, ALU.mult), reads=[b_sm], writes=[b_sm])
        S.op("dve", I("tensor_tensor", sm[:, 3:4], sm[:, 2:3], sm[:, 2:3], ALU.mult), reads=[b_sm], writes=[b_sm])
        S.op("dve", I("scalar_tensor_tensor", sm[:, 4:5], sm[:, 1:2], 1.0 / D, sm[:, 3:4], ALU.mult, ALU.subtract),
             reads=[b_sm], writes=[b_sm])
        S.op("dve", I("tensor_scalar", sm[:, 4:5], sm[:, 4:5], 0.0, None, ALU.max), reads=[b_sm], writes=[b_sm])
        rstd_from_sumsq(4, 5, 1, 1.0)
        g = vecbc[:, gcol * D:(gcol + 1) * D]
        bb = vecbc[:, bcol * D:(bcol + 1) * D]
        S.op("dve", I("scalar_tensor_tensor", xout, xin, sm[:, 2:3], g, ALU.subtract, ALU.mult),
             reads=[b_in, b_sm, b_vec], writes=[b_out])
        S.op("dve", I("scalar_tensor_tensor", xout, xout, sm[:, 5:6], bb, ALU.mult, ALU.add),
             reads=[b_out, b_sm, b_vec], writes=[b_out])

    def proj_fm(wname, srcT, b_src):
        ps_y, pb_y = pp()
        for hv in range(2):
            wt, wb_ = wstream(wname, 0, KD, hv * 512, 512)
            for m in range(4):
                n = hv * 4 + m
                S.op("pe", mm_group([(ps_y[:, n * 128:(n + 1) * 128], wt[:, kc * 512 + m * 128: kc * 512 + (m + 1) * 128],
                                      srcT[:, kc * 128:(kc + 1) * 128], kc == 0, kc == KD - 1) for kc in range(KD)]),
                     reads=[b_src, wb_], writes=[pb_y[hv]])
        return ps_y, pb_y

    def proj_tm(wname, srcT, b_src, nkc=KD, base_only=False):
        ps_y, pb_y = pp()
        for hv in range(2):
            wt, wb_ = wstream(wname, 0, nkc, hv * 512, 512, base_only=base_only)
            S.op("pe", mm_group([(ps_y[:, hv * 512:(hv + 1) * 512], srcT[:, kc * 128:(kc + 1) * 128], wt[:, kc * 512:(kc + 1) * 512],
                                  kc == 0, kc == nkc - 1) for kc in range(nkc)]), reads=[b_src, wb_], writes=[pb_y[hv]])
        return ps_y, pb_y

    def dsa_scores_and_setup(f):
        alias_close()
        snk128 = sinkD[:, 0:1].broadcast_to([128, 128])
        nk = (f + 1) * 128
        nch = (nk + 511) // 512
        iqT = Z[:, 2048:2560]
        dgs = []
        for hd in range(8):
            qt, b_qt = (Q0, b_Q0) if hd < 4 else (Q1, b_Q1)
            dg = qt[:, (hd % 4) * 128:(hd % 4 + 1) * 128]
            S.op("dve", I("tensor_scalar", dg, identb[:], sm[:, 16 + hd:17 + hd], None, ALU.mult), reads=[b_idb, b_sm], writes=[b_qt])
            dgs.append((dg, b_qt))
        iqz = OTn[:, 0:1024]
        for hd in range(8):
            S.op("dve", I("tensor_scalar", iqz[:, hd * 128:(hd + 1) * 128], iqT[:, (hd // 2) * 128:(hd // 2 + 1) * 128],
                          cstt[:, O_SEL + hd % 2:O_SEL + hd % 2 + 1], None, ALU.mult), reads=[b_Z, b_cst], writes=[b_OTn])
        R0b = R0[:].bitcast(BF16)
        R1b = R1[:].bitcast(BF16)
        rbs = [(R0b[:, 0:512], b_rb[0], b_R0), (R0b[:, 512:1024], b_rb[1], b_R0), (R1b[:, 0:512], b_rb[2], b_R1), (R1b[:, 512:1024], b_rb[3], b_R1)]
        psi = [(PP[k // 2][:, (k % 2) * 512:(k % 2 + 1) * 512], [PB[k // 2][k % 2]]) for k in range(6)]
        psa = [(PP[3][:, 0:512], [PB[3][0]]), (PP[3][:, 512:1024], [PB[3][1]])]
        items = [(c, hd) for c in range(nch) for hd in range(8)]
        LAI = 3

        def emit_idx(n):
            c, hd = items[n]
            cw = min(512, nk - c * 512)
            ps_i, pb_i = psi[n % 6]
            rb, b_r, b_whole = rbs[n % 4]
            first_use = (n < 4)
            S.op("pe", I("matmul", ps_i[:, 0:cw], iqz[:, hd * 128:(hd + 1) * 128], ikT2[:, c * 512:c * 512 + cw],
                         start=True, stop=True), reads=[b_OTn, b_ikT2], writes=pb_i)
            if n % 2 == 0:
                S.op("act", I("activation", rb[:, 0:cw], ps_i[:, 0:cw], AF.Relu), reads=pb_i, writes=[b_r] + ([b_whole] if first_use else []))
            else:
                S.op("dve", I("tensor_scalar", rb[:, 0:cw], ps_i[:, 0:cw], 0.0, None, ALU.max), reads=pb_i,
                     writes=[b_r] + ([b_whole] if first_use else []))

        for n in range(min(LAI, len(items))):
            emit_idx(n)
        for n, (c, hd) in enumerate(items):
            if n + LAI < len(items):
                emit_idx(n + LAI)
            cw = min(512, nk - c * 512)
            ps_acc, pb_acc = psa[c % 2]
            rb, b_r, b_whole = rbs[n % 4]
            last_use = (n >= len(items) - 4)
            dg, b_dg = dgs[hd]
            S.op("pe", I("matmul", ps_acc[:, 0:cw], dg, rb[:, 0:cw], start=(hd == 0), stop=(hd == 7)),
                 reads=[b_dg, b_r] + ([b_whole] if last_use else []), writes=pb_acc)
            if hd == 7:
                sc = score[:, c * 512:c * 512 + cw]
                if c < G // 4:
                    S.op("dve", I("tensor_tensor", sc, ps_acc[:, 0:cw], kvbt[:, 0:cw], ALU.add), reads=pb_acc + [b_kvb], writes=[b_score])
                else:
                    S.op("dve", I("tensor_copy", sc, ps_acc[:, 0:cw]), reads=pb_acc, writes=[b_score])
        snkN = sinkD[:, 0:1].broadcast_to([128, nk])
        dcol = f * 128
        S.op("dve", I("memset", bis[:, 0:4], BIG), writes=[b_bis])
        nfirst = min(f, G) * 128
        if nfirst > 0:
            S.op("dve", I("tensor_scalar", sinkD[:, 0:1].broadcast_to([128, nfirst]), score[:, 0:nfirst], kvbt[:, 512:513], None,
                                                  ALU.add, ALU.min, accum_out=bis[:, 0:1]), reads=[b_score, b_kvb, b_bis], writes=[b_bis])
        if f > G:
            S.op("dve", I("tensor_scalar", sinkD[:, 0:1].broadcast_to([128, (f - G) * 128]), score[:, G * 128:f * 128], 1.0, None,
                                                  ALU.mult, ALU.min, accum_out=bis[:, 1:2]), reads=[b_score, b_bis], writes=[b_bis])
        dtmp, b_dtmp = fa[0]
        S.op("dve", I("tensor_tensor", dtmp[:], score[:, dcol:dcol + 128], C(O_DBP), ALU.add), reads=[b_score, b_cst], writes=[b_dtmp])
        S.op("dve", I("tensor_scalar", snk128, dtmp[:], 1.0, None, ALU.mult, ALU.min, accum_out=bis[:, 2:3]),
             reads=[b_dtmp, b_bis], writes=[b_bis])
        S.op("dve", I("tensor_tensor", score[:, dcol:dcol + 128], score[:, dcol:dcol + 128], C(O_DB), ALU.add),
             reads=[b_score, b_cst], writes=[b_score])
        S.op("dve", I("tensor_scalar", snkN, score[:, 0:nk], 1.0, None, ALU.mult, ALU.max, accum_out=bis[:, 5:6]),
             reads=[b_score, b_bis], writes=[b_bis])
        S.op("dve", I("tensor_tensor", bis[:, 4:5], bis[:, 0:1], bis[:, 1:2], ALU.min), reads=[b_bis], writes=[b_bis])
        S.op("dve", I("tensor_tensor", bis[:, 4:5], bis[:, 4:5], bis[:, 2:3], ALU.min), reads=[b_bis], writes=[b_bis])
        S.op("dve", I("tensor_tensor", bis[:, 7:8], bis[:, 5:6], bis[:, 4:5], ALU.subtract), reads=[b_bis], writes=[b_bis])
        S.op("dve", I("scalar_tensor_tensor", bis[:, 6:7], bis[:, 7:8], 0.5, bis[:, 4:5], ALU.mult, ALU.add), reads=[b_bis], writes=[b_bis, b_c])
        S.op("dve", I("tensor_scalar", hcol[:, 0:NIT + 1], cstt[:, O_P2:O_P2 + NIT + 1], bis[:, 7:8], None, ALU.mult),
             reads=[b_bis, b_cst], writes=[b_hcol])

    def gen_bis(f):
        nk = (f + 1) * 128
        n1 = int(round(0.4 * nk / 128.0)) * 128
        if nk < 1024:
            n1 = nk
        n2 = nk - n1
        snk1 = sinkD[:, 0:1].broadcast_to([128, n1])
        for it in range(NIT):
            S.op("dve", I("tensor_scalar", snk1, score[:, 0:n1], bis[:, 6:7], None, ALU.is_ge, ALU.add, accum_out=bis[:, 8:9]),
                 reads=[b_score, b_c], writes=[b_cntD])
            if n2 > 0:
                S.op("act", I("activation", sinkA[:, 0:1].broadcast_to([128, n2]), score[:, n1:nk], AF.Sign, bias=bis[:, 6:7], scale=-1.0,
                              accum_out=bis[:, 11:12]), reads=[b_score, b_c], writes=[b_cntA])
                S.op("dve", I("scalar_tensor_tensor", bis[:, 12:13], bis[:, 11:12], -0.5, bis[:, 8:9], ALU.mult, ALU.add),
                     reads=[b_cntD, b_cntA], writes=[b_btmp])
                S.op("dve", I("tensor_scalar", bis[:, 9:10], bis[:, 12:13], topk - 0.5 - 0.5 * n2, 0.5, ALU.is_ge, ALU.subtract),
                     reads=[b_btmp], writes=[b_btmp])
            else:
                S.op("dve", I("tensor_scalar", bis[:, 9:10], bis[:, 8:9], topk - 0.5, 0.5, ALU.is_ge, ALU.subtract), reads=[b_cntD], writes=[b_btmp])
            S.op("dve", I("scalar_tensor_tensor", bis[:, 6:7], bis[:, 9:10], hcol[:, it:it + 1], bis[:, 6:7], ALU.mult, ALU.add),
                 reads=[b_btmp, b_hcol, b_c], writes=[b_c])
            yield
        S.op("dve", I("scalar_tensor_tensor", bis[:, 10:11], hcol[:, NIT:NIT + 1], -1.25, bis[:, 6:7], ALU.mult, ALU.add),
             reads=[b_c, b_hcol], writes=[b_thr])
        yield

    def interleave(gens):
        gens = list(gens)
        while gens:
            for g_ in list(gens):
                try:
                    next(g_)
                except StopIteration:
                    gens.remove(g_)

    def attention_merge_ln1(f, halo, grp):
        nk = (f + 1) * 128
        nkb = f + 1
        PSo, PBo = PP[0][:, :], PB[0]
        PSd, PBd = PP[1][:, :], PB[1]
        ring = [(PP[2][:, 0:512], [PB[2][0]]), (PP[2][:, 512:1024], [PB[2][1]]), (PP[3][:, 0:512], [PB[3][0]]), (PP[3][:, 512:1024], [PB[3][1]])]
        free = [0, 1, 2, 3]
        mks = [(Q0, b_Q0), (Q1, b_Q1)]
        mTs = [(Q2, b_Q2), (Q3, b_Q3)]
        R2h = R2[:].bitcast(BF16)
        PMs = [PM[0], PM[1], (R2h[:, 0:512], b_pmx[0]), (R2h[:, 512:1024], b_pmx[1])]
        for hg in range(2):
            slot_of = {}

            def emit_mask(c):
                cw = min(512, nk - c * 512)
                nb4 = cw // 128
                mk, b_mk = mks[c % 2]
                mT, b_mT = mTs[c % 2]
                S.op("dve", I("tensor_scalar", mk[:, 0:cw], score[:, c * 512:c * 512 + cw], bis[:, 10:11], None, ALU.is_ge),
                     reads=[b_score, b_thr], writes=[b_mk])
                sl = free.pop(0)
                PSm, PBm = ring[sl]
                PSmb = PSm.bitcast(BF16)

                def ft(e, mk=mk, nb4=nb4, PSmb=PSmb):
                    ins = None
                    for q in range(nb4):
                        ins = e.transpose(PSmb[:, q * 128:(q + 1) * 128], mk[:, q * 128:(q + 1) * 128], identb[:])
                    return ins
                S.op("pe", ft, reads=[b_mk, b_idb], writes=PBm)
                S.op("act", I("copy", mT[:, 0:cw], PSmb[:, 0:cw]), reads=PBm, writes=[b_mT])
                free.append(sl)

            def emit_qk_unit(kb):
                if kb % 4 == 0:
                    emit_mask(kb // 4)
                items = []
                wr = []
                for j in range(2):
                    sl = free.pop(0)
                    slot_of[(kb, j)] = sl
                    ps, pb_ = ring[sl]
                    items.append((ps, ckvT[:, kb * 128:(kb + 1) * 128], Z[:, hg * 1024 + j * 512: hg * 1024 + (j + 1) * 512], True, True))
                    wr += pb_
                S.op("pe", mm_group(items), reads=[b_ckvT, b_Z], writes=wr)

            emit_qk_unit(0)
            if nkb > 1:
                emit_qk_unit(1)
            for kb in range(nkb):
                pts = []
                for j in range(2):
                    sl = slot_of.pop((kb, j))
                    ps, pb_ = ring[sl]
                    pT, b_pT = PT[j]
                    S.op("act", I("activation", pT[:], ps, AF.Exp, scale=128.0 ** -0.5), reads=pb_, writes=[b_pT])
                    free.append(sl)
                if kb + 2 < nkb:
                    emit_qk_unit(kb + 2)
                mT, b_mT = mTs[(kb // 4) % 2]
                kq = kb % 4
                items = []
                rd = []
                for j in range(2):
                    pT, b_pT = PT[j]
                    pm, b_pm = PMs[(kb % 2) * 2 + j]
                    extra = [b_R2] if (kb % 2 == 1 and kb < 4) else []
                    S.op(MASK_ENG, I("tensor_tensor", pm[:, :].rearrange("p (h t) -> p h t", h=4), pT[:].rearrange("p (h t) -> p h t", h=4),
                                     mT[:, kq * 128:(kq + 1) * 128].unsqueeze(1).broadcast_to([128, 4, 128]), ALU.mult),
                         reads=[b_pT, b_mT], writes=[b_pm] + extra)
                    rd.append(b_pm)
                for j in range(2):
                    pm, b_pm = PMs[(kb % 2) * 2 + j]
                    items.append((PSo[:, j * 512:(j + 1) * 512], ckvS[:, kb * 128:(kb + 1) * 128], pm[:, :], kb == 0, kb == nkb - 1))
                for j in range(2):
                    pm, b_pm = PMs[(kb % 2) * 2 + j]
                    items.append((PSd[:, j * 512:(j + 1) * 512], onesb[:], pm[:, :], kb == 0, kb == nkb - 1))
                last = [b_R2] if kb >= nkb - 2 else []
                S.op("pe", mm_group(items), reads=[b_ckvS, b_ones] + rd + last, writes=PBo + PBd)
            rden, b_rden = FB, b_FB
            S.op("dve", I("tensor_scalar", rden[:], PSd, 1e-30, None, ALU.max), reads=PBd, writes=[b_rden])
            S.op("dve", I("reciprocal", rden[:], rden[:]), reads=[b_rden], writes=[b_rden])
            S.op("dve", I("tensor_tensor", OTn[:, hg * 1024:(hg + 1) * 1024], PSo, rden[:], ALU.mult),
                 reads=PBo + [b_rden], writes=[b_OTn])
        alias_open()
        ps_ob, pb_ob = pp()
        for hp in range(8):
            S.op("pe", mm_group([(ps_ob[:, hp * 128:(hp + 1) * 128], wuvb[:, (2 * hp) * 128:(2 * hp + 1) * 128], OTn[:, (2 * hp) * 128:(2 * hp + 1) * 128], True, False),
                                 (ps_ob[:, hp * 128:(hp + 1) * 128], wuvb[:, (2 * hp + 1) * 128:(2 * hp + 2) * 128], OTn[:, (2 * hp + 1) * 128:(2 * hp + 2) * 128], False, True)]),
                 reads=[b_wuv, b_OTn], writes=[pb_ob[hp // 4]])
        o_bT, b_obT = B2, b_B2
        S.op("act", I("copy", o_bT[:], ps_ob), reads=pb_ob, writes=[b_obT])

        o_aT, b_oaT = B3, b_B3
        ps_ya, pb_ya = proj_fm("wgp", o_aT, b_oaT)
        t1, b_t1 = FB, b_FB
        S.op("dve", I("tensor_tensor", t1[:], ps_ya, sgT[:, 0:1024], ALU.mult), reads=pb_ya + [b_sgT], writes=[b_t1])
        ps_yb, pb_yb = proj_fm("wdp", o_bT, b_obT)
        t2, b_t2 = FC, b_FC
        S.op("dve", I("tensor_tensor", t2[:], ps_yb, sgT[:, 1024:2048], ALU.mult), reads=pb_yb + [b_sgT], writes=[b_t2])
        mgT, b_mgT = B0, b_B0
        S.op("dve", I("tensor_tensor", mgT[:], t1[:], t2[:], ALU.add), reads=[b_t1, b_t2], writes=[b_mgT])

        ps_mx, pb_mx = proj_tm("wout", mgT, b_mgT)
        S.op("dve", I("scalar_tensor_tensor", FA[:], FA[:], ALPHA, ps_mx, ALU.mult, ALU.add), reads=[b_FA] + pb_mx, writes=[b_FA])
        x1, b_x1 = FE, b_FE
        layer_norm(FA[:], b_FA, x1[:], b_x1, V_L1G, V_L1B)
        x1b, b_x1b = B1, b_B1
        S.op("pool", I("tensor_copy", x1b[:], x1[:]), reads=[b_x1], writes=[b_x1b])
        ps_t, pb_t = ph()
        pstb = ps_t.bitcast(BF16)

        def ftx(e, pstb=pstb, x1b=x1b):
            ins = None
            for k in range(KD):
                ins = e.transpose(pstb[:, k * 128:(k + 1) * 128], x1b[:, k * 128:(k + 1) * 128], identb[:])
            return ins
        S.op("pe", ftx, reads=[b_x1b, b_idb], writes=pb_t)
        x1v = x1Te[:].rearrange("p (k t) -> p k t", k=KD)
        S.op("act", I("copy", x1v[:, :, 2:130], pstb.rearrange("p (k t) -> p k t", k=KD)), reads=pb_t, writes=[b_x1Te])
        if f == 0 or True:
            dbg("x1_%d" % f, x1[:], [b_x1])
        if halo:
            m = grp // 2
            S.op("dve", I("tensor_scalar", x1v[:, :, 0:2], x1v[:, :, 128:130], hflt[:, m:m + 1], None, ALU.mult),
                 reads=[b_x1Te, b_hfl], writes=[b_x1Te])
            return


    def gen_D(f, oi):
        x1, b_x1 = FE, b_FE
        x1v = x1Te[:].rearrange("p (k t) -> p k t", k=KD)
        actT, b_actT = OTn, b_OTn
        for sc_ in range(11):
            wt, wb_ = wstream("wup", 0, KD, sc_ * 512, 512, base_only=True)
            res = []
            for r in range(4):
                cc = sc_ * 4 + r
                ps_h, pb_h = ph()
                S.op("pe", mm_group([(ps_h[:, 0:130], wt[:, kc * 512 + r * 128: kc * 512 + (r + 1) * 128], x1Te[:, kc * 130:(kc + 1) * 130],
                                      kc == 0, kc == KD - 1) for kc in range(KD)]), reads=[b_x1Te, wb_], writes=pb_h)
                a, b_a = fa[r]
                S.op("act", I("activation", a[:], ps_h[:, 2:130], AF.Identity, bias=convt[:, cc * 4 + 3:cc * 4 + 4],
                                                                       scale=convt[:, cc * 4 + 2:cc * 4 + 3]), reads=pb_h + [b_conv], writes=[b_a])
                S.op("dve", I("scalar_tensor_tensor", a[:], ps_h[:, 1:129], convt[:, cc * 4 + 1:cc * 4 + 2], a[:], ALU.mult, ALU.add),
                     reads=pb_h + [b_conv, b_a], writes=[b_a])
                S.op("dve", I("scalar_tensor_tensor", a[:], ps_h[:, 0:128], convt[:, cc * 4:cc * 4 + 1], a[:], ALU.mult, ALU.add),
                     reads=pb_h + [b_conv, b_a], writes=[b_a])
                res.append((a, b_a))
            for r in range(2):
                ga_, b_ga = res[r]
                va_, b_va = res[2 + r]
                q = sc_ * 2 + r
                S.op("act", I("activation", ga_[:], ga_[:], AF.Silu), reads=[b_ga], writes=[b_ga])
                S.op("pool", I("tensor_tensor", actT[:, q * 128:(q + 1) * 128], ga_[:], va_[:], ALU.mult),
                     reads=[b_ga, b_va], writes=[b_actT])
            yield
        S.op("dve", I("tensor_copy", x1v[:, :, 0:2], x1v[:, :, 128:130]), reads=[b_x1Te], writes=[b_x1Te])
        ps_ff, pb_ff = pp()
        for hv in range(2):
            groups = [(0, 8), (8, 8), (16, 6)]
            for gi, (k0, nkc) in enumerate(groups):
                wt, wb_ = wstream("wdn", k0, nkc, hv * 512, 512, base_only=True)
                S.op("pe", mm_group([(ps_ff[:, hv * 512:(hv + 1) * 512], actT[:, (k0 + kc) * 128:(k0 + kc + 1) * 128], wt[:, kc * 512:(kc + 1) * 512],
                                      (gi == 0 and kc == 0), (gi == 2 and kc == nkc - 1)) for kc in range(nkc)]),
                     reads=[b_actT, wb_], writes=[pb_ff[hv]])
                yield
        x2p, b_x2p = FD, b_FD
        S.op("dve", I("scalar_tensor_tensor", x2p[:], x1[:], ALPHA, ps_ff, ALU.mult, ALU.add), reads=[b_x1] + pb_ff, writes=[b_x2p])
        x2, b_x2 = FC, b_FC
        layer_norm(x2p[:], b_x2p, x2[:], b_x2, V_L2G, V_L2B)
        yield
        x2b, b_x2b = B0, b_B0
        S.op("pool", I("tensor_copy", x2b[:], x2[:]), reads=[b_x2], writes=[b_x2b])
        x2T, b_x2T = B1, b_B1
        transposes8(x2b, b_x2b, x2T[:], b_x2T)
        yield
        ps_pg, pb_pg = proj_tm("wpg", x2T, b_x2T, base_only=True)
        sg2, b_sg2 = FB, b_FB
        S.op("act", I("activation", sg2[:], ps_pg, AF.Sigmoid), reads=pb_pg, writes=[b_sg2])
        yield
        p0 = oi * 128
        S.dma("pool", I("dma_start", out=pbt[:], in_=pown[p0:p0 + 128, :]), writes=[b_pbt])
        transposes8(pbt, b_pbt, pTt[:], b_pTt, nch=2)
        ps_pw, pb_pw = proj_tm("wple", pTt, b_pTt, nkc=2, base_only=True)
        S.op("dve", I("tensor_tensor", sg2[:], sg2[:], ps_pw, ALU.mult), reads=[b_sg2] + pb_pw, writes=[b_sg2])
        x3p, b_x3p = FD, b_FD
        S.op("dve", I("scalar_tensor_tensor", x3p[:], x2[:], ALPHA, sg2[:], ALU.mult, ALU.add), reads=[b_x2, b_sg2], writes=[b_x3p])
        layer_norm(x3p[:], b_x3p, FD[:], b_FD, V_L3G, V_L3B)
        S.dma("sp", I("dma_start", out=yown[p0:p0 + 128, :], in_=FD[:]), reads=[b_FD], is_output=True)

        yield

    pendingD = [None]
    for f in range(NB):
        grp = f // G
        own = (grp % 2 == 1)
        halo = (not own) and (f % G == G - 1)
        full = own or halo
        oi = (grp // 2) * G + f % G
        xb, b_xb = xbs[0]
        r0 = f * 128
        S.dma("pool", I("dma_start", out=xb[:], in_=xfr[r0:r0 + 128, :]), writes=[b_xb])
        if full:
            S.dma("sp", I("dma_start", out=FA[:], in_=xfr[r0:r0 + 128, :]), writes=[b_FA])
        flush_precast(len(pending_precast) if f >= min(8, G - 2) else 14)
        xT, b_xT = B1, b_B1
        transposes8(xb, b_xb, xT[:], b_xT)

        def xTk(kc):
            return xT[:, kc * 128:(kc + 1) * 128]

        wt, wb_ = wstream("win", 0, KD, C_K, 512)
        ps_k, pb_k = ph()
        S.op("pe", mm_group([(ps_k, xTk(kc), wt[:, kc * 512:(kc + 1) * 512], kc == 0, kc == KD - 1) for kc in range(KD)]),
             reads=[b_xT, wb_], writes=pb_k)
        k_tm, b_ktm = R0, b_R0
        S.op("act", I("copy", k_tm[:], ps_k), reads=pb_k, writes=[b_ktm])
        if full:
            ps_kT, pb_kT = ph()
            for hh in range(4):
                S.op("pe", mm_group([(ps_kT[:, hh * 128:(hh + 1) * 128], wt[:, kc * 512 + hh * 128: kc * 512 + (hh + 1) * 128], xTk(kc),
                                      kc == 0, kc == KD - 1) for kc in range(KD)]), reads=[b_xT, wb_], writes=pb_kT)
            S.op("act", I("copy", Q2[:], ps_kT), reads=pb_kT, writes=[b_Q2])
        ps_v, pb_v = pp()
        for hv in range(2):
            wt, wb_ = wstream("win", 0, KD, C_V + hv * 512, 512)
            S.op("pe", mm_group([(ps_v[:, hv * 512:(hv + 1) * 512], xTk(kc), wt[:, kc * 512:(kc + 1) * 512], kc == 0, kc == KD - 1)
                                 for kc in range(KD)]), reads=[b_xT, wb_], writes=[pb_v[hv]])
        v, b_v = B2, b_B2
        S.op("act", I("copy", v[:], ps_v), reads=pb_v, writes=[b_v])
        wt, wb_ = wstream("win", 0, KD, C_SM, 280)
        ps_s, pb_s = ph()
        S.op("pe", mm_group([(ps_s[:, 0:128], xTk(kc), wt[:, kc * 280: kc * 280 + 128], kc == 0, kc == KD - 1) for kc in range(KD)]),
             reads=[b_xT, wb_], writes=pb_s)
        if full:
            S.op("pe", mm_group([(ps_s[:, 128:136], xTk(kc), wt[:, kc * 280 + 272: kc * 280 + 280], kc == 0, kc == KD - 1)
                                 for kc in range(KD)]), reads=[b_xT, wb_], writes=pb_s)
        ps_f, pb_f = ph()
        S.op("pe", mm_group([(ps_f[:, 0:128], wt[:, kc * 280 + 128: kc * 280 + 256], xTk(kc), kc == 0, kc == KD - 1) for kc in range(KD)]),
             reads=[b_xT, wb_], writes=pb_f)
        S.op("pe", mm_group([(ps_f[0:16, 128:256], wt[:, kc * 280 + 256: kc * 280 + 272], xTk(kc), kc == 0, kc == KD - 1)
                             for kc in range(KD)]), reads=[b_xT, wb_], writes=pb_f)
        S.op("act", I("copy", ikT2[:, r0:r0 + 128], ps_f[:, 0:128]), reads=pb_f, writes=[b_ikT2])
        S.op("act", I("copy", gaT[0:16, :], ps_f[0:16, 128:256]), reads=pb_f, writes=[b_gaT])
        snk128 = sinkA[:, 0:1].broadcast_to([128, 128])
        S.op("act", I("activation", snk128, ps_s[:, 0:128], AF.Square, accum_out=sm[:, 8:9]), reads=pb_s, writes=[b_sm])
        rstd_from_sumsq(8, 9, 1, 1.0 / 128)
        S.op("dve", I("scalar_tensor_tensor", ckvS[:, r0:r0 + 128], ps_s[:, 0:128], sm[:, 9:10], gckvbc[:], ALU.mult, ALU.mult),
             reads=pb_s + [b_sm, b_gckv], writes=[b_ckvS])
        if full:
            S.op("dve", I("tensor_scalar", sm[:, 16:24], ps_s[:, 128:136], (8.0 ** -0.5) * (64.0 ** -0.5), None, ALU.mult),
                 reads=pb_s, writes=[b_sm])
        ps_t, pb_t = ph()
        pstb = ps_t.bitcast(BF16)
        S.op("pe", I("transpose", pstb[:, 0:128], ckvS[:, r0:r0 + 128], identb[:]), reads=[b_ckvS, b_idb], writes=pb_t)
        S.op("act", I("copy", ckvT[:, r0:r0 + 128], pstb[:, 0:128]), reads=pb_t, writes=[b_ckvT])

        ps_z, pb_z = ph()
        S.op("pe", I("matmul", ps_z, gaT[:, :], wgub[:, :], start=True, stop=True), reads=[b_gaT, b_wgu], writes=pb_z)
        Lt, b_L = R1, b_R1
        S.op("act", I("activation", Lt[:], ps_z, AF.Exp, scale=-1.0), reads=pb_z, writes=[b_L])
        S.op("act", I("activation", Lt[:], Lt[:], AF.Ln, bias=1.0), reads=[b_L], writes=[b_L])
        ps_e, pb_e = ph()
        S.op("pe", I("matmul", ps_e, C(O_UN), Lt[:], start=True, stop=True), reads=[b_cst, b_L], writes=pb_e)
        ps_d, pb_d = ph()
        for hh in range(4):
            S.op("pe", I("matmul", ps_d[:, hh:hh + 1], Lt[:, hh * 128:(hh + 1) * 128], cstt[:, O_TRIN + 127:O_TRIN + 128],
                                                 start=True, stop=True), reads=[b_cst, b_L], writes=pb_d)
        S.op("act", I("activation", sm[:, 24:28], ps_d[:, 0:4], AF.Exp), reads=pb_d, writes=[b_sm])
        est, b_est = R2, b_R2
        S.op("act", I("activation", est[:], ps_e, AF.Exp), reads=pb_e, writes=[b_est])
        kst, b_kst = Q0, b_Q0
        S.op("dve", I("tensor_tensor", kst[:], k_tm[:], est[:], ALU.mult), reads=[b_ktm, b_est], writes=[b_kst])

        if full:
            wt, wb_ = wstream("win", 0, KD, C_Q, 512)
            ps_qT, pb_qT = ph()
            for hh in range(4):
                S.op("pe", mm_group([(ps_qT[:, hh * 128:(hh + 1) * 128], wt[:, kc * 512 + hh * 128: kc * 512 + (hh + 1) * 128], xTk(kc),
                                      kc == 0, kc == KD - 1) for kc in range(KD)]), reads=[b_xT, wb_], writes=pb_qT)
            ps_b, pb_b = ph()
            for hh in range(4):
                S.op("pe", I("matmul", ps_b[:, hh * 128:(hh + 1) * 128], Lt[:, hh * 128:(hh + 1) * 128], C(O_TRIN),
                                                     start=True, stop=True), reads=[b_cst, b_L], writes=pb_b)
            S.op("act", I("activation", est[:], ps_b, AF.Exp), reads=pb_b, writes=[b_est])
            qin, b_qin = Q1, b_Q1
            S.op("dve", I("scalar_tensor_tensor", qin[:], ps_qT, 128.0 ** -0.5, est[:], ALU.mult, ALU.mult),
                 reads=pb_qT + [b_est], writes=[b_qin])
            S.op("act", I("activation", est[:], ps_b, AF.Exp, scale=-1.0), reads=pb_b + [b_qin], writes=[b_est])
            kin, b_kin = Q2, b_Q2
            S.op("dve", I("tensor_tensor", kin[:], kin[:], est[:], ALU.mult), reads=[b_kin, b_est], writes=[b_kin])
            ps_g, pb_g = pp()
            for hv in range(2):
                wt, wb_ = wstream("win", 0, KD, C_GR + hv * 512, 512)
                S.op("pe", mm_group([(ps_g[:, hv * 512:(hv + 1) * 512], xTk(kc), wt[:, kc * 512:(kc + 1) * 512], kc == 0, kc == KD - 1)
                                     for kc in range(KD)]), reads=[b_xT, wb_], writes=[pb_g[hv]])
            sgr, b_sgr = FB, b_FB
            S.op("act", I("activation", sgr[:], ps_g, AF.Silu), reads=pb_g, writes=[b_sgr])
            for ch in range(5):
                wt, wb_ = wstream("win", 0, KD, (C_DQ + ch * 512) if ch < 4 else C_IQ, 512)
                ps_q, pb_q = ph()
                for m in range(4):
                    S.op("pe", mm_group([(ps_q[:, m * 128:(m + 1) * 128], wt[:, kc * 512 + m * 128: kc * 512 + (m + 1) * 128], xTk(kc),
                                          kc == 0, kc == KD - 1) for kc in range(KD)]), reads=[b_xT, wb_], writes=pb_q)
                S.op("act", I("copy", Z[:, ch * 512:(ch + 1) * 512], ps_q), reads=pb_q, writes=[b_Z])
            for ch in range(4):
                wt, wb_ = wstream("win", 0, KD, C_GT + ch * 512, 512)
                ps_q, pb_q = ph()
                for m in range(4):
                    S.op("pe", mm_group([(ps_q[:, m * 128:(m + 1) * 128], wt[:, kc * 512 + m * 128: kc * 512 + (m + 1) * 128], xTk(kc),
                                          kc == 0, kc == KD - 1) for kc in range(KD)]), reads=[b_xT, wb_], writes=pb_q)
                S.op("act", I("activation", sgT[:, ch * 512:(ch + 1) * 512], ps_q, AF.Sigmoid),
                     reads=pb_q, writes=[b_sgT])

            ps_a, pb_a = ph()
            for hh in range(4):
                S.op("pe", I("matmul", ps_a[:, hh * 128:(hh + 1) * 128], kin[:, hh * 128:(hh + 1) * 128],
                                                     qin[:, hh * 128:(hh + 1) * 128], start=True, stop=True),
                     reads=[b_kin, b_qin], writes=pb_a)
            attT, b_att = Q3, b_Q3
            S.op("dve", I("tensor_tensor", attT[:].rearrange("p (h t) -> p h t", h=4), ps_a.rearrange("p (h t) -> p h t", h=4),
                                                  C(O_TRI01).unsqueeze(1).broadcast_to([128, 4, 128]), ALU.mult),
                 reads=pb_a + [b_cst], writes=[b_att])
            ps_o, pb_o = pp()
            for hh in range(4):
                S.op("pe", mm_group([(ps_o[:, hh * 256:(hh + 1) * 256], attT[:, hh * 128:(hh + 1) * 128], v[:, hh * 256:(hh + 1) * 256], True, False),
                                     (ps_o[:, hh * 256:(hh + 1) * 256], qin[:, hh * 128:(hh + 1) * 128], Sbf[:, hh * 256:(hh + 1) * 256], False, True)]),
                     reads=[b_att, b_v, b_qin, b_Sbf], writes=[pb_o[hh // 2]])
            snk256 = sinkA[:, 0:1].broadcast_to([128, 256])
            for hh in range(4):
                S.op("act", I("activation", snk256, ps_o[:, hh * 256:(hh + 1) * 256], AF.Square, accum_out=sm[:, 28 + hh:29 + hh]),
                     reads=[pb_o[hh // 2]], writes=[b_sm])
            rstd_from_sumsq(28, 32, 4, 1.0 / 256)
            o_n, b_on = FC, b_FC
            for hh in range(4):
                S.op("dve", I("scalar_tensor_tensor", o_n[:, hh * 256:(hh + 1) * 256], ps_o[:, hh * 256:(hh + 1) * 256],
                                                                    sm[:, 32 + hh:33 + hh], vecbc[:, V_GN * D + hh * 256: V_GN * D + (hh + 1) * 256],
                                                                    ALU.mult, ALU.mult),
                     reads=[pb_o[hh // 2], b_sm, b_vec], writes=[b_on])
            o_an, b_oan = B0, b_B0
            S.op("dve", I("tensor_tensor", o_an[:], o_n[:], sgr[:], ALU.mult), reads=[b_on, b_sgr], writes=[b_oan])
            o_aT, b_oaT = B3, b_B3
            transposes8(o_an, b_oan, o_aT[:], b_oaT)

        ps_kv, pb_kv = pp()
        for hh in range(4):
            S.op("pe", I("matmul", ps_kv[:, hh * 256:(hh + 1) * 256], kst[:, hh * 128:(hh + 1) * 128], v[:, hh * 256:(hh + 1) * 256],
                                                 start=True, stop=True), reads=[b_kst, b_v], writes=[pb_kv[hh // 2]])
        for hh in range(4):
            S.op("dve", I("scalar_tensor_tensor", Sst[:, hh * 256:(hh + 1) * 256], Sst[:, hh * 256:(hh + 1) * 256],
                                                                sm[:, 24 + hh:25 + hh], ps_kv[:, hh * 256:(hh + 1) * 256], ALU.mult, ALU.add),
                 reads=[b_S, b_sm, pb_kv[hh // 2]], writes=[b_S])
        S.op("pool", I("tensor_copy", Sbf[:], Sst[:]), reads=[b_S], writes=[b_Sbf])

        if not full:
            continue
        dsa_scores_and_setup(f)
        gens = [gen_bis(f)]
        if pendingD[0] is not None:
            gens.append(pendingD[0])
            pendingD[0] = None
        interleave(gens)
        attention_merge_ln1(f, halo, grp)
        if own:
            pendingD[0] = gen_D(f, oi)
    if pendingD[0] is not None:
        for _ in pendingD[0]:
            pass

    S.finish()
    S.emit()
    S.close()
    return nc


SPLIT = (512, 512, 1024, 1024, 16, 2048, 128, 512, 64, 8, 2048)


def host_consts(NIT):
    tok = np.arange(128)
    cst = np.zeros((128, NCST), np.float32)
    cst[:, O_ID:O_ID + 128] = np.eye(128, dtype=np.float32)
    cst[:, O_TRIN:O_TRIN + 128] = np.where(tok[:, None] <= tok[None, :], -1.0 / 16, 0.0)
    cst[:, O_UN:O_UN + 128] = np.where(tok[:, None] > tok[None, :], -1.0 / 16, 0.0)
    cst[:, O_TRI01:O_TRI01 + 128] = np.where(tok[:, None] <= tok[None, :], 1.0, 0.0)
    cst[:, O_DB:O_DB + 128] = np.where(tok[None, :] <= tok[:, None], 0.0, -BIG)
    cst[:, O_DBP:O_DBP + 128] = np.where(tok[None, :] <= tok[:, None], 0.0, BIG)
    cst[:, O_P2:O_P2 + NIT + 1] = (2.0 ** -(np.arange(NIT + 1) + 1.0))[None, :]
    cst[0:64, O_SEL] = 1.0
    cst[64:128, O_SEL + 1] = 1.0
    return cst


def prep_weights(w):
    offs = np.cumsum((0,) + SPLIT)
    seg = {n: (int(offs[i]), int(offs[i + 1])) for i, n in enumerate(
        ["gq", "gk", "gv", "gr", "ga", "dq", "ckv", "iq", "ik", "iw", "gates"])}
    wi = w["w_in"]

    def cols(n):
        a, b = seg[n]
        return wi[:, a:b]
    win2 = np.concatenate([cols("gk"), cols("gv"), cols("ckv"), cols("ik"), cols("ik"), cols("ga"), cols("iw"),
                           cols("gq"), cols("gr"), cols("dq"), cols("iq"), cols("gates")], axis=1)
    assert win2.shape[1] == NC2
    wgu = np.concatenate([w["w_gla_gate_up"], w["b_gla_gate"][None, :]], axis=0)
    wup = w["w_up"]
    pieces = []
    chan = []
    for i in range(11):
        pieces.append(wup[:, 256 * i:256 * i + 256])
        pieces.append(wup[:, DFF + 256 * i:DFF + 256 * i + 256])
        chan += [256 * i, 256 * i + 128, DFF + 256 * i, DFF + 256 * i + 128]
    wup2 = np.concatenate(pieces, axis=1)
    convp = np.zeros((128, NCC * 4), np.float32)
    for cc, ch0 in enumerate(chan):
        convp[:, cc * 4 + 0] = w["conv_w"][0, ch0:ch0 + 128]
        convp[:, cc * 4 + 1] = w["conv_w"][1, ch0:ch0 + 128]
        convp[:, cc * 4 + 2] = w["conv_w"][2, ch0:ch0 + 128]
        convp[:, cc * 4 + 3] = w["conv_b"][ch0:ch0 + 128]
    wuvp = np.zeros((128, 16, 128), np.float32)
    for h in range(16):
        wuvp[:, h, (h % 2) * 64:(h % 2) * 64 + 64] = w["w_uv"][h]
    vecs = np.stack([w["g_gla_norm"], w["ln1_g"], w["ln1_b"], w["ln2_g"], w["ln2_b"], w["ln3_g"], w["ln3_b"]], axis=0)
    f32 = lambda a: np.ascontiguousarray(a, dtype=np.float32)
    return dict(win2=f32(win2), wgu=f32(wgu), wgp=f32(w["w_gla_proj"]), wdp=f32(w["w_dsa_proj"]), wout=f32(w["w_out"]),
                wup2=f32(wup2), wdn=f32(w["w_down"]), wpg=f32(w["w_ple_gate"]), wple=f32(w["w_ple"]),
                wuvp=f32(wuvp.reshape(128, 2048)), vecs=f32(vecs), gckv=f32(w["g_ckv_norm"][None, :]), convp=f32(convp))


def run_module(x, p, w, G, NIT=16, dbg_names=()):
    B, L, _ = x.shape
    NB = L // 128
    assert NB % (2 * G) == 0
    topk = min(256, L // 4)
    nc = build_program(L, G, topk, NIT=NIT, dbg_names=dbg_names)
    shared = prep_weights(w)
    shared["cst"] = host_consts(NIT)
    own_frames = [f for f in range(NB) if (f // G) % 2 == 1]
    in_maps = []
    for b in range(B):
        for h in range(2):
            m = dict(shared)
            shift = G * (1 - h)
            xf = np.zeros((L, D), np.float32)
            xf[shift * 128:] = x[b, :L - shift * 128]
            m["xfr"] = xf
            m["pown"] = np.ascontiguousarray(np.concatenate([p[b, (f - shift) * 128:(f - shift + 1) * 128] for f in own_frames], axis=0))
            kv = np.zeros((128, 520), np.float32)
            if h == 0:
                kv[:, 0:512] = -BIG
                kv[:, 512:520] = 3.0 * BIG
            m["kvb"] = kv
            hf = np.ones((128, 8), np.float32)
            if h == 0:
                hf[:, 0] = 0.0
            m["hflag"] = hf
            in_maps.append(m)
    res = run_bass_kernel_spmd(nc, in_maps, core_ids=list(range(2 * B)))
    out = np.empty((B, L, D), np.float32)
    for b in range(B):
        for h in range(2):
            shift = G * (1 - h)
            y = res.results[b * 2 + h]["yown"]
            for oi, f in enumerate(own_frames):
                g = f - shift
                out[b, g * 128:(g + 1) * 128] = y[oi * 128:(oi + 1) * 128]
    dbgs = [{k: v for k, v in r.items() if k.startswith("dbg_")} for r in res.results]
    return out, dbgs


def kernel(**inputs):
    x = np.asarray(inputs["x"], dtype=np.float32)
    p = np.asarray(inputs["p"], dtype=np.float32)[0]
    names = ["w_in", "w_gla_gate_up", "b_gla_gate", "g_gla_norm", "w_gla_proj", "g_ckv_norm", "w_uv", "w_dsa_proj", "w_out",
             "ln1_g", "ln1_b", "w_up", "conv_w", "conv_b", "w_down", "ln2_g", "ln2_b", "w_ple", "w_ple_gate", "ln3_g", "ln3_b"]
    w = {n: np.asarray(inputs[n], dtype=np.float32)[0] for n in names}
    out, _ = run_module(x, p, w, G=16)
    return out
```

```python
import contextlib
import numpy as np
import concourse.bass as bass
import concourse.mybir as mybir
from concourse.bass_utils import run_bass_kernel_spmd

F32 = mybir.dt.float32
BF16 = mybir.dt.bfloat16
ALU = mybir.AluOpType
AF = mybir.ActivationFunctionType

ENGS = ("pe", "act", "dve", "pool", "sp")


class Buf:
    __slots__ = ("name", "w", "r")

    def __init__(self, name):
        self.name = name
        self.w = None
        self.r = {}


class Sched:
    EPOCH = 12000
    NDMA = 24

    def __init__(self, nc):
        self.nc = nc
        self.ops = {e: [] for e in ENGS}
        self.ccount = {e: 0 for e in ENGS}
        self.waited = {e: {} for e in ENGS}
        self.ndma = 0
        self.out_dma_tokens = []
        self.stack = contextlib.ExitStack()
        self.sems = {}
        self.nbuf = 0

    def sbuf(self, name, shape, dtype):
        return self.stack.enter_context(self.nc.sbuf_tensor(name, list(shape), dtype))

    def psum(self, name, shape, dtype):
        return self.stack.enter_context(self.nc.psum_tensor(name, list(shape), dtype))

    def buf(self, name=None):
        self.nbuf += 1
        return Buf(name or f"b{self.nbuf}")

    def _sem(self, key):
        if key not in self.sems:
            nm = "s_" + "_".join(str(k) for k in key)
            self.sems[key] = self.stack.enter_context(self.nc.semaphore(nm))
        return self.sems[key]

    def _tokwait(self, tok):
        if tok[0] == "e":
            _, eng, idx = tok
            return (eng, idx // self.EPOCH), idx % self.EPOCH + 1
        _, k = tok
        return ("dma", k % self.NDMA), 16 * (k // self.NDMA + 1)

    def _resolve(self, eng, deps):
        need = {}
        for tok in deps:
            key, val = self._tokwait(tok)
            if val > need.get(key, 0):
                need[key] = val
        waits = []
        wd = self.waited[eng]
        for key, val in need.items():
            if wd.get(key, 0) >= val:
                continue
            wd[key] = val
            waits.append((key, val))
        return waits

    def _deps(self, reads, writes):
        deps = set()
        for b in reads:
            if b.w is not None:
                deps.add(b.w)
        for b in writes:
            if b.w is not None:
                deps.add(b.w)
            deps.update(b.r.values())
        return deps

    def _mark(self, tok, rkey, reads, writes):
        for b in reads:
            b.r[rkey] = tok
        for b in writes:
            b.w = tok
            b.r = {}

    def op(self, eng, fn, reads=(), writes=()):
        idx = self.ccount[eng]
        tok = ("e", eng, idx)
        waits = self._resolve(eng, self._deps(reads, writes))
        self.ccount[eng] = idx + 1
        self._mark(tok, eng, reads, writes)
        self.ops[eng].append((fn, waits, ((eng, idx // self.EPOCH), 1)))
        return tok

    def dma(self, eng, fn, reads=(), writes=(), is_output=False):
        k = self.ndma
        self.ndma += 1
        tok = ("d", k)
        deps = self._deps(reads, writes)
        if k >= self.NDMA:
            deps.add(("d", k - self.NDMA))
        waits = self._resolve(eng, deps)
        self._mark(tok, ("d", k), reads, writes)
        self.ops[eng].append((fn, waits, (("dma", k % self.NDMA), 16)))
        if is_output:
            self.out_dma_tokens.append(tok)
        return tok

    def finish(self, eng="sp"):
        waits = self._resolve(eng, set(self.out_dma_tokens))
        self.ops[eng].append((None, waits, None))

    def emit(self):
        nc = self.nc
        for e in ENGS:
            for fn, waits, inc in self.ops[e]:
                for key, _ in waits:
                    self._sem(key)
                if inc is not None:
                    self._sem(inc[0])
        ops = self.ops
        sems = self.sems

        def replay(e, engine):
            for fn, waits, inc in ops[e]:
                for key, val in waits:
                    engine.wait_ge(sems[key], val)
                if fn is None:
                    continue
                ins = fn(engine)
                ins.then_inc(sems[inc[0]], inc[1])

        with nc.Block() as block:
            @block.tensor
            def _(eng):
                replay("pe", eng)

            @block.scalar
            def _(eng):
                replay("act", eng)

            @block.vector
            def _(eng):
                replay("dve", eng)

            @block.gpsimd
            def _(eng):
                replay("pool", eng)

            @block.sync
            def _(eng):
                replay("sp", eng)

    def close(self):
        self.stack.close()


def I(method, *args, **kw):
    return lambda e: getattr(e, method)(*args, **kw)


D = 1024
KD = 8
DFF = 2816
NCC = 44
PLE = 256
BIG = 1.0e30
ALPHA = 2.0 ** 0.25
EPS = 1e-5
C_K, C_V, C_SM, C_Q, C_GR, C_DQ, C_IQ, C_GT = 0, 512, 1536, 1816, 2328, 3352, 5400, 5912
NC2 = 7960
O_ID, O_TRIN, O_UN, O_TRI01, O_DB, O_DBP, O_P2 = 0, 128, 256, 384, 512, 640, 768
O_SEL = 832
NCST = 840
V_GN, V_L1G, V_L1B, V_L2G, V_L2B, V_L3G, V_L3B = range(7)


def build_program(L, G, topk, NIT=16, dbg_names=()):
    NB = L // 128
    NOWN = NB // 2
    NGRP = NB // G
    assert NGRP % 2 == 0 and G % 4 == 0
    nc = bass.Bass("TRN2", target_bir_lowering=False)

    def din(name, shape, dt=F32):
        return nc.dram_tensor(name, list(shape), dt, kind="ExternalInput").ap()

    xfr = din("xfr", [L, D])
    pown = din("pown", [NOWN * 128, PLE])
    win2 = din("win2", [D, NC2])
    wgu = din("wgu", [17, 512])
    wgp = din("wgp", [D, D])
    wdp = din("wdp", [D, D])
    wout = din("wout", [D, D])
    wup2 = din("wup2", [D, 2 * DFF])
    wdn = din("wdn", [DFF, D])
    wpg = din("wpg", [D, D])
    wple = din("wple", [PLE, D])
    wuvp = din("wuvp", [128, 2048])
    vecs = din("vecs", [7, D])
    gckv = din("gckv", [1, 128])
    convp = din("convp", [128, NCC * 4])
    cst = din("cst", [128, NCST])
    kvb = din("kvb", [128, 520])
    hflag = din("hflag", [128, 8])
    yown = nc.dram_tensor("yown", [NOWN * 128, D], F32, kind="ExternalOutput").ap()
    dbg_out = {}
    for nm, shp in dbg_names:
        dbg_out[nm] = nc.dram_tensor("dbg_" + nm, list(shp), F32, kind="ExternalOutput").ap()

    def dscr(name, shape):
        return nc.dram_tensor(name, list(shape), BF16, kind="Internal").ap()

    S = Sched(nc)

    wsrc = {"win": (win2, D, NC2), "wgp": (wgp, D, D), "wdp": (wdp, D, D), "wout": (wout, D, D),
            "wup": (wup2, D, 2 * DFF), "wdn": (wdn, DFF, D), "wpg": (wpg, D, D), "wple": (wple, PLE, D)}
    wchunks = {
        "win": [(0, KD, c0, w) for (c0, w) in [(C_K, 512), (C_V, 512), (C_V + 512, 512), (C_SM, 280), (C_Q, 512), (C_GR, 512), (C_GR + 512, 512)]
                + [(C_DQ + i * 512, 512) for i in range(4)] + [(C_IQ, 512)] + [(C_GT + i * 512, 512) for i in range(4)]],
        "wgp": [(0, KD, 0, 512), (0, KD, 512, 512)],
        "wdp": [(0, KD, 0, 512), (0, KD, 512, 512)],
        "wout": [(0, KD, 0, 512), (0, KD, 512, 512)],
        "wup": [(0, KD, i * 512, 512) for i in range(11)],
        "wdn": [(k0, nkc, hv * 512, 512) for hv in range(2) for (k0, nkc) in [(0, 8), (8, 8), (16, 6)]],
        "wpg": [(0, KD, 0, 512), (0, KD, 512, 512)],
        "wple": [(0, 2, 0, 512), (0, 2, 512, 512)],
    }
    wb16 = {}
    wcb = {}
    pending_precast = []
    for nm, chs in wchunks.items():
        src = wsrc[nm][0]
        for q, (k0, nkc, c0, w_) in enumerate(chs):
            dst = dscr(f"b16_{nm}_{q}", [128, nkc * w_])
            wb16[(nm, k0, nkc, c0, w_)] = dst
            bl = []
            for kc in range(nkc):
                bb_ = S.buf(f"{nm}_{q}_{kc}")

                def f(e, dst=dst, src=src, k0=k0, kc=kc, c0=c0, w_=w_):
                    return e.dma_start(out=dst[:, kc * w_:(kc + 1) * w_], in_=src[(k0 + kc) * 128:(k0 + kc + 1) * 128, c0:c0 + w_])
                pending_precast.append((f, bb_))
                bl.append(bb_)
            wcb[(nm, k0, nkc, c0, w_)] = bl

    def flush_precast(n):
        for _ in range(min(n, len(pending_precast))):
            f, bb_ = pending_precast.pop(0)
            S.dma("pool", f, writes=[bb_])

    def tile(name, cols, dt=F32, parts=128):
        return S.sbuf(name, [parts, cols], dt), S.buf(name)

    LK = NB * 128
    ckvT, b_ckvT = tile("ckvT", LK, BF16)
    ckvS, b_ckvS = tile("ckvS", LK, BF16)
    ikT2, b_ikT2 = tile("ikT2", LK, BF16)
    score, b_score = tile("score", LK, F32)
    vecbc, b_vec = tile("vecbc", 7 * D, F32)
    gckvbc, b_gckv = tile("gckvbc", 128, F32)
    cstt, b_cst = tile("cstt", NCST, F32)
    identb, b_idb = tile("identb", 128, BF16)
    onesb, b_ones = tile("onesb", 128, BF16)
    kvbt, b_kvb = tile("kvbt", 520, F32)
    hflt, b_hfl = tile("hflt", 8, F32)
    convt, b_conv = tile("convt", NCC * 4, F32)
    wgub, b_wgu = tile("wgub", 512, BF16, parts=17)
    wuvb, b_wuv = tile("wuvb", 2048, BF16)
    Sst, b_S = tile("Sst", 1024, F32)
    Sbf, b_Sbf = tile("Sbf", 1024, BF16)
    x1Te, b_x1Te = tile("x1Te", KD * 130, BF16)
    NW = 2
    wst = [tile(f"wst{i}", 4096, BF16) for i in range(NW)]
    xbs = [tile(f"xb{i}", D, BF16) for i in range(1)]
    FA, b_FA = tile("FA", D, F32)
    FB, b_FB = tile("FB", D, F32)
    FC, b_FC = tile("FC", D, F32)
    FD, b_FD = tile("FD", D, F32)
    B0, b_B0 = tile("B0", D, BF16)
    B1, b_B1 = tile("B1", D, BF16)
    B2, b_B2 = tile("B2", D, BF16)
    B3, b_B3 = tile("B3", D, BF16)
    R0, b_R0 = tile("R0", 512, F32)
    R1, b_R1 = tile("R1", 512, F32)
    R2, b_R2 = tile("R2", 512, F32)
    Q0, b_Q0 = tile("Q0", 512, BF16)
    Q1, b_Q1 = tile("Q1", 512, BF16)
    Q2, b_Q2 = tile("Q2", 512, BF16)
    Q3, b_Q3 = tile("Q3", 512, BF16)
    gaT, b_gaT = tile("gaT", 128, BF16, parts=17)
    Z, b_Z = tile("Z", DFF, BF16)
    sgT, b_sgT = tile("sgT", 2048, BF16)
    OTn, b_OTn = tile("OTn", DFF, BF16)
    fa = [tile(f"fa{i}", 128, F32) for i in range(4)]
    pbt, b_pbt = tile("pbt", 256, BF16)
    pTt, b_pTt = tile("pTt", 256, BF16)
    sm, b_sm = tile("sm", 64, F32)
    bis, b_bis = tile("bis", 64, F32)
    hcol, b_hcol = tile("hcol", 32, F32)
    sinkA = S.sbuf("sinkA", [128, 8], BF16)
    sinkD = S.sbuf("sinkD", [128, 8], BF16)
    FE, b_FE = tile("FE", D, F32)
    PT = [tile(f"PT{i}", 512, BF16) for i in range(2)]
    PM = [tile(f"PM{i}", 512, BF16) for i in range(2)]
    MASK_ENG = "dve"
    b_rb = [S.buf(f"rb{i}") for i in range(4)]
    b_pmx = [S.buf(f"pmx{i}") for i in range(2)]
    b_c, b_cntD, b_cntA, b_btmp, b_thr = S.buf("b_c"), S.buf("b_cntD"), S.buf("b_cntA"), S.buf("b_btmp"), S.buf("b_thr")

    PP = [S.psum(f"pp{i}", [128, 1024], F32) for i in range(4)]
    PB = [[S.buf(f"pp{i}_{j}") for j in range(2)] for i in range(4)]
    rr = {"h": 0, "p": 0}

    def ph():
        i = rr["h"] % 8
        rr["h"] += 1
        return PP[i // 2][:, (i % 2) * 512:(i % 2 + 1) * 512], [PB[i // 2][i % 2]]

    def pp():
        i = rr["p"] % 4
        rr["p"] += 1
        return PP[i][:, :], PB[i]

    def dbg(nm, ap, bufs):
        if nm in dbg_out:
            o = dbg_out[nm]
            S.dma("sp", I("dma_start", out=o[:, :], in_=ap), reads=bufs, is_output=True)

    S.dma("sp", I("dma_start", out=cstt[:], in_=cst[:, :]), writes=[b_cst])
    S.dma("sp", I("dma_start", out=kvbt[:], in_=kvb[:, :]), writes=[b_kvb])
    S.dma("sp", I("dma_start", out=hflt[:], in_=hflag[:, :]), writes=[b_hfl])
    S.dma("sp", I("dma_start", out=convt[:], in_=convp[:, :]), writes=[b_conv])
    S.dma("sp", I("dma_start", out=gckvbc[:], in_=gckv[0:1, :].partition_broadcast(128)), writes=[b_gckv])
    for r in range(7):
        S.dma("sp", I("dma_start", out=vecbc[:, r * D:(r + 1) * D], in_=vecs[r:r + 1, :].partition_broadcast(128)),
              writes=[b_vec])
    S.dma("pool", I("dma_start", out=wgub[:], in_=wgu[:, :]), writes=[b_wgu])
    S.dma("pool", I("dma_start", out=wuvb[:], in_=wuvp[:, :]), writes=[b_wuv])
    S.dma("pool", I("dma_start", out=identb[:], in_=cst[:, O_ID:O_ID + 128]), writes=[b_idb])
    flush_precast(4 * KD)
    S.op("dve", I("memset", onesb[:], 1.0), writes=[b_ones])
    S.op("dve", I("memset", Sst[:], 0.0), writes=[b_S])
    S.op("dve", I("memset", Sbf[:], 0.0), writes=[b_Sbf])
    S.op("dve", I("memset", gaT[:], 1.0), writes=[b_gaT])
    S.op("dve", I("memset", x1Te[:], 0.0), writes=[b_x1Te])

    def C(off, n=128):
        return cstt[:, off:off + n]

    wrot = {"i": 0}

    NAL = min(4, LK // 2048)
    b_al = [S.buf(f"alias{k}") for k in range(NAL)]
    al_slots = [(score[:, k * 2048:(k + 1) * 2048].bitcast(BF16), b_al[k]) for k in range(NAL)]
    al = {"ok": True, "i": 0}

    def alias_open():
        S.op("dve", I("memset", sinkD[:, 4:5], 0.0), reads=[b_score], writes=b_al)
        al["ok"] = True

    def alias_close():
        al["ok"] = False
        S.op("dve", I("memset", sinkD[:, 5:6], 0.0), writes=b_al + [b_score])

    def wstream(nm, kc0, nkc, c0, ncols, base_only=False):
        if al["ok"] and not base_only:
            i = al["i"] % (NW + NAL)
            al["i"] += 1
            t, b = (wst[i][0][:, :], wst[i][1]) if i < NW else al_slots[i - NW]
        else:
            i = wrot["i"] % NW
            wrot["i"] += 1
            t, b = wst[i][0][:, :], wst[i][1]
        key = (nm, kc0, nkc, c0, ncols)
        S.dma("sp", I("dma_start", out=t[:, 0:nkc * ncols], in_=wb16[key][:, :]), reads=wcb[key], writes=[b])
        return t, b

    def mm_group(items):
        def f(e):
            ins = None
            for (o, l, r, st, sp) in items:
                ins = e.matmul(o, l, r, start=st, stop=sp)
            return ins
        return f

    def transposes8(src, b_src, dst, b_dst, ncols=128, nch=8, eng_copy="act"):
        ps, pb_ = ph()
        psb = ps.bitcast(BF16)

        def f(e):
            ins = None
            for k in range(nch):
                ins = e.transpose(psb[:, k * 128:(k + 1) * 128], src[:, k * 128:(k + 1) * 128], identb[:])
            return ins
        S.op("pe", f, reads=[b_src, b_idb], writes=pb_)
        if eng_copy == "act":
            S.op("act", I("copy", dst, psb[:, 0:nch * 128]), reads=pb_, writes=[b_dst])
        else:
            S.op("dve", I("tensor_copy", dst, psb[:, 0:nch * 128]), reads=pb_, writes=[b_dst])

    def rstd_from_sumsq(col_in, col_out, n, inv_count):
        S.op("act", I("activation", sm[:, col_out:col_out + n], sm[:, col_in:col_in + n], AF.Ln, bias=EPS, scale=inv_count),
             reads=[b_sm], writes=[b_sm])
        S.op("act", I("activation", sm[:, col_out:col_out + n], sm[:, col_out:col_out + n], AF.Exp, scale=-0.5),
             reads=[b_sm], writes=[b_sm])

    def layer_norm(xin, b_in, xout, b_out, gcol, bcol):
        snk = sinkA[:, 0:1].broadcast_to([128, D])
        S.op("act", I("activation", snk, xin, AF.Identity, accum_out=sm[:, 0:1]), reads=[b_in], writes=[b_sm])
        S.op("act", I("activation", snk, xin, AF.Square, accum_out=sm[:, 1:2]), reads=[b_in], writes=[b_sm])
        S.op("dve", I("tensor_scalar", sm[:, 2:3], sm[:, 0:1], 1.0 / D, None, ALU.mult), reads=[b_sm], writes=[b_sm])
        S.op("dve", I("tensor_tensor", sm[:, 3:4], sm[:, 2:3], sm[:, 2:3], ALU.mult), reads=[b_sm], writes=[b_sm])
        S.op("dve", I("scalar_tensor_tensor", sm[:, 4:5], sm[:, 1:2], 1.0 / D, sm[:, 3:4], ALU.mult, ALU.subtract),
             reads=[b_sm], writes=[b_sm])
        S.op("dve", I("tensor_scalar", sm[:, 4:5], sm[:, 4:5], 0.0, None, ALU.max), reads=[b_sm], writes=[b_sm])
        rstd_from_sumsq(4, 5, 1, 1.0)
        g = vecbc[:, gcol * D:(gcol + 1) * D]
        bb = vecbc[:, bcol * D:(bcol + 1) * D]
        S.op("dve", I("scalar_tensor_tensor", xout, xin, sm[:, 2:3], g, ALU.subtract, ALU.mult),
             reads=[b_in, b_sm, b_vec], writes=[b_out])
        S.op("dve", I("scalar_tensor_tensor", xout, xout, sm[:, 5:6], bb, ALU.mult, ALU.add),
             reads=[b_out, b_sm, b_vec], writes=[b_out])

    def proj_fm(wname, srcT, b_src):
        ps_y, pb_y = pp()
        for hv in range(2):
            wt, wb_ = wstream(wname, 0, KD, hv * 512, 512)
            for m in range(4):
                n = hv * 4 + m
                S.op("pe", mm_group([(ps_y[:, n * 128:(n + 1) * 128], wt[:, kc * 512 + m * 128: kc * 512 + (m + 1) * 128],
                                      srcT[:, kc * 128:(kc + 1) * 128], kc == 0, kc == KD - 1) for kc in range(KD)]),
                     reads=[b_src, wb_], writes=[pb_y[hv]])
        return ps_y, pb_y

    def proj_tm(wname, srcT, b_src, nkc=KD, base_only=False):
        ps_y, pb_y = pp()
        for hv in range(2):
            wt, wb_ = wstream(wname, 0, nkc, hv * 512, 512, base_only=base_only)
            S.op("pe", mm_group([(ps_y[:, hv * 512:(hv + 1) * 512], srcT[:, kc * 128:(kc + 1) * 128], wt[:, kc * 512:(kc + 1) * 512],
                                  kc == 0, kc == nkc - 1) for kc in range(nkc)]), reads=[b_src, wb_], writes=[pb_y[hv]])
        return ps_y, pb_y

    def dsa_scores_and_setup(f):
        alias_close()
        snk128 = sinkD[:, 0:1].broadcast_to([128, 128])
        nk = (f + 1) * 128
        nch = (nk + 511) // 512
        iqT = Z[:, 2048:2560]
        dgs = []
        for hd in range(8):
            qt, b_qt = (Q0, b_Q0) if hd < 4 else (Q1, b_Q1)
            dg = qt[:, (hd % 4) * 128:(hd % 4 + 1) * 128]
            S.op("dve", I("tensor_scalar", dg, identb[:], sm[:, 16 + hd:17 + hd], None, ALU.mult), reads=[b_idb, b_sm], writes=[b_qt])
            dgs.append((dg, b_qt))
        iqz = OTn[:, 0:1024]
        for hd in range(8):
            S.op("dve", I("tensor_scalar", iqz[:, hd * 128:(hd + 1) * 128], iqT[:, (hd // 2) * 128:(hd // 2 + 1) * 128],
                          cstt[:, O_SEL + hd % 2:O_SEL + hd % 2 + 1], None, ALU.mult), reads=[b_Z, b_cst], writes=[b_OTn])
        R0b = R0[:].bitcast(BF16)
        R1b = R1[:].bitcast(BF16)
        rbs = [(R0b[:, 0:512], b_rb[0], b_R0), (R0b[:, 512:1024], b_rb[1], b_R0), (R1b[:, 0:512], b_rb[2], b_R1), (R1b[:, 512:1024], b_rb[3], b_R1)]
        psi = [(PP[k // 2][:, (k % 2) * 512:(k % 2 + 1) * 512], [PB[k // 2][k % 2]]) for k in range(6)]
        psa = [(PP[3][:, 0:512], [PB[3][0]]), (PP[3][:, 512:1024], [PB[3][1]])]
        items = [(c, hd) for c in range(nch) for hd in range(8)]
        LAI = 3

        def emit_idx(n):
            c, hd = items[n]
            cw = min(512, nk - c * 512)
            ps_i, pb_i = psi[n % 6]
            rb, b_r, b_whole = rbs[n % 4]
            first_use = (n < 4)
            S.op("pe", I("matmul", ps_i[:, 0:cw], iqz[:, hd * 128:(hd + 1) * 128], ikT2[:, c * 512:c * 512 + cw],
                         start=True, stop=True), reads=[b_OTn, b_ikT2], writes=pb_i)
            if n % 2 == 0:
                S.op("act", I("activation", rb[:, 0:cw], ps_i[:, 0:cw], AF.Relu), reads=pb_i, writes=[b_r] + ([b_whole] if first_use else []))
            else:
                S.op("dve", I("tensor_scalar", rb[:, 0:cw], ps_i[:, 0:cw], 0.0, None, ALU.max), reads=pb_i,
                     writes=[b_r] + ([b_whole] if first_use else []))

        for n in range(min(LAI, len(items))):
            emit_idx(n)
        for n, (c, hd) in enumerate(items):
            if n + LAI < len(items):
                emit_idx(n + LAI)
            cw = min(512, nk - c * 512)
            ps_acc, pb_acc = psa[c % 2]
            rb, b_r, b_whole = rbs[n % 4]
            last_use = (n >= len(items) - 4)
            dg, b_dg = dgs[hd]
            S.op("pe", I("matmul", ps_acc[:, 0:cw], dg, rb[:, 0:cw], start=(hd == 0), stop=(hd == 7)),
                 reads=[b_dg, b_r] + ([b_whole] if last_use else []), writes=pb_acc)
            if hd == 7:
                sc = score[:, c * 512:c * 512 + cw]
                if c < G // 4:
                    S.op("dve", I("tensor_tensor", sc, ps_acc[:, 0:cw], kvbt[:, 0:cw], ALU.add), reads=pb_acc + [b_kvb], writes=[b_score])
                else:
                    S.op("dve", I("tensor_copy", sc, ps_acc[:, 0:cw]), reads=pb_acc, writes=[b_score])
        snkN = sinkD[:, 0:1].broadcast_to([128, nk])
        dcol = f * 128
        S.op("dve", I("memset", bis[:, 0:4], BIG), writes=[b_bis])
        nfirst = min(f, G) * 128
        if nfirst > 0:
            S.op("dve", I("tensor_scalar", sinkD[:, 0:1].broadcast_to([128, nfirst]), score[:, 0:nfirst], kvbt[:, 512:513], None,
                                                  ALU.add, ALU.min, accum_out=bis[:, 0:1]), reads=[b_score, b_kvb, b_bis], writes=[b_bis])
        if f > G:
            S.op("dve", I("tensor_scalar", sinkD[:, 0:1].broadcast_to([128, (f - G) * 128]), score[:, G * 128:f * 128], 1.0, None,
                                                  ALU.mult, ALU.min, accum_out=bis[:, 1:2]), reads=[b_score, b_bis], writes=[b_bis])
        dtmp, b_dtmp = fa[0]
        S.op("dve", I("tensor_tensor", dtmp[:], score[:, dcol:dcol + 128], C(O_DBP), ALU.add), reads=[b_score, b_cst], writes=[b_dtmp])
        S.op("dve", I("tensor_scalar", snk128, dtmp[:], 1.0, None, ALU.mult, ALU.min, accum_out=bis[:, 2:3]),
             reads=[b_dtmp, b_bis], writes=[b_bis])
        S.op("dve", I("tensor_tensor", score[:, dcol:dcol + 128], score[:, dcol:dcol + 128], C(O_DB), ALU.add),
             reads=[b_score, b_cst], writes=[b_score])
        S.op("dve", I("tensor_scalar", snkN, score[:, 0:nk], 1.0, None, ALU.mult, ALU.max, accum_out=bis[:, 5:6]),
             reads=[b_score, b_bis], writes=[b_bis])
        S.op("dve", I("tensor_tensor", bis[:, 4:5], bis[:, 0:1], bis[:, 1:2], ALU.min), reads=[b_bis], writes=[b_bis])
        S.op("dve", I("tensor_tensor", bis[:, 4:5], bis[:, 4:5], bis[:, 2:3], ALU.min), reads=[b_bis], writes=[b_bis])
        S.op("dve", I("tensor_tensor", bis[:, 7:8], bis[:, 5:6], bis[:, 4:5], ALU.subtract), reads=[b_bis], writes=[b_bis])
        S.op("dve", I("scalar_tensor_tensor", bis[:, 6:7], bis[:, 7:8], 0.5, bis[:, 4:5], ALU.mult, ALU.add), reads=[b_bis], writes=[b_bis, b_c])
        S.op("dve", I("tensor_scalar", hcol[:, 0:NIT + 1], cstt[:, O_P2:O_P2 + NIT + 1], bis[:, 7:8], None, ALU.mult),
             reads=[b_bis, b_cst], writes=[b_hcol])

    def gen_bis(f):
        nk = (f + 1) * 128
        n1 = int(round(0.4 * nk / 128.0)) * 128
        if nk < 1024:
            n1 = nk
        n2 = nk - n1
        snk1 = sinkD[:, 0:1].broadcast_to([128, n1])
        for it in range(NIT):
            S.op("dve", I("tensor_scalar", snk1, score[:, 0:n1], bis[:, 6:7], None, ALU.is_ge, ALU.add, accum_out=bis[:, 8:9]),
                 reads=[b_score, b_c], writes=[b_cntD])
            if n2 > 0:
                S.op("act", I("activation", sinkA[:, 0:1].broadcast_to([128, n2]), score[:, n1:nk], AF.Sign, bias=bis[:, 6:7], scale=-1.0,
                              accum_out=bis[:, 11:12]), reads=[b_score, b_c], writes=[b_cntA])
                S.op("dve", I("scalar_tensor_tensor", bis[:, 12:13], bis[:, 11:12], -0.5, bis[:, 8:9], ALU.mult, ALU.add),
                     reads=[b_cntD, b_cntA], writes=[b_btmp])
                S.op("dve", I("tensor_scalar", bis[:, 9:10], bis[:, 12:13], topk - 0.5 - 0.5 * n2, 0.5, ALU.is_ge, ALU.subtract),
                     reads=[b_btmp], writes=[b_btmp])
            else:
                S.op("dve", I("tensor_scalar", bis[:, 9:10], bis[:, 8:9], topk - 0.5, 0.5, ALU.is_ge, ALU.subtract), reads=[b_cntD], writes=[b_btmp])
            S.op("dve", I("scalar_tensor_tensor", bis[:, 6:7], bis[:, 9:10], hcol[:, it:it + 1], bis[:, 6:7], ALU.mult, ALU.add),
                 reads=[b_btmp, b_hcol, b_c], writes=[b_c])
            yield
        S.op("dve", I("scalar_tensor_tensor", bis[:, 10:11], hcol[:, NIT:NIT + 1], -1.25, bis[:, 6:7], ALU.mult, ALU.add),
             reads=[b_c, b_hcol], writes=[b_thr])
        yield

    def interleave(gens):
        gens = list(gens)
        while gens:
            for g_ in list(gens):
                try:
                    next(g_)
                except StopIteration:
                    gens.remove(g_)

    def attention_merge_ln1(f, halo, grp):
        nk = (f + 1) * 128
        nkb = f + 1
        PSo, PBo = PP[0][:, :], PB[0]
        PSd, PBd = PP[1][:, :], PB[1]
        ring = [(PP[2][:, 0:512], [PB[2][0]]), (PP[2][:, 512:1024], [PB[2][1]]), (PP[3][:, 0:512], [PB[3][0]]), (PP[3][:, 512:1024], [PB[3][1]])]
        free = [0, 1, 2, 3]
        mks = [(Q0, b_Q0), (Q1, b_Q1)]
        mTs = [(Q2, b_Q2), (Q3, b_Q3)]
        R2h = R2[:].bitcast(BF16)
        PMs = [PM[0], PM[1], (R2h[:, 0:512], b_pmx[0]), (R2h[:, 512:1024], b_pmx[1])]
        for hg in range(2):
            slot_of = {}

            def emit_mask(c):
                cw = min(512, nk - c * 512)
                nb4 = cw // 128
                mk, b_mk = mks[c % 2]
                mT, b_mT = mTs[c % 2]
                S.op("dve", I("tensor_scalar", mk[:, 0:cw], score[:, c * 512:c * 512 + cw], bis[:, 10:11], None, ALU.is_ge),
                     reads=[b_score, b_thr], writes=[b_mk])
                sl = free.pop(0)
                PSm, PBm = ring[sl]
                PSmb = PSm.bitcast(BF16)

                def ft(e, mk=mk, nb4=nb4, PSmb=PSmb):
                    ins = None
                    for q in range(nb4):
                        ins = e.transpose(PSmb[:, q * 128:(q + 1) * 128], mk[:, q * 128:(q + 1) * 128], identb[:])
                    return ins
                S.op("pe", ft, reads=[b_mk, b_idb], writes=PBm)
                S.op("act", I("copy", mT[:, 0:cw], PSmb[:, 0:cw]), reads=PBm, writes=[b_mT])
                free.append(sl)

            def emit_qk_unit(kb):
                if kb % 4 == 0:
                    emit_mask(kb // 4)
                items = []
                wr = []
                for j in range(2):
                    sl = free.pop(0)
                    slot_of[(kb, j)] = sl
                    ps, pb_ = ring[sl]
                    items.append((ps, ckvT[:, kb * 128:(kb + 1) * 128], Z[:, hg * 1024 + j * 512: hg * 1024 + (j + 1) * 512], True, True))
                    wr += pb_
                S.op("pe", mm_group(items), reads=[b_ckvT, b_Z], writes=wr)

            emit_qk_unit(0)
            if nkb > 1:
                emit_qk_unit(1)
            for kb in range(nkb):
                pts = []
                for j in range(2):
                    sl = slot_of.pop((kb, j))
                    ps, pb_ = ring[sl]
                    pT, b_pT = PT[j]
                    S.op("act", I("activation", pT[:], ps, AF.Exp, scale=128.0 ** -0.5), reads=pb_, writes=[b_pT])
                    free.append(sl)
                if kb + 2 < nkb:
                    emit_qk_unit(kb + 2)
                mT, b_mT = mTs[(kb // 4) % 2]
                kq = kb % 4
                items = []
                rd = []
                for j in range(2):
                    pT, b_pT = PT[j]
                    pm, b_pm = PMs[(kb % 2) * 2 + j]
                    extra = [b_R2] if (kb % 2 == 1 and kb < 4) else []
                    S.op(MASK_ENG, I("tensor_tensor", pm[:, :].rearrange("p (h t) -> p h t", h=4), pT[:].rearrange("p (h t) -> p h t", h=4),
                                     mT[:, kq * 128:(kq + 1) * 128].unsqueeze(1).broadcast_to([128, 4, 128]), ALU.mult),
                         reads=[b_pT, b_mT], writes=[b_pm] + extra)
                    rd.append(b_pm)
                for j in range(2):
                    pm, b_pm = PMs[(kb % 2) * 2 + j]
                    items.append((PSo[:, j * 512:(j + 1) * 512], ckvS[:, kb * 128:(kb + 1) * 128], pm[:, :], kb == 0, kb == nkb - 1))
                for j in range(2):
                    pm, b_pm = PMs[(kb % 2) * 2 + j]
                    items.append((PSd[:, j * 512:(j + 1) * 512], onesb[:], pm[:, :], kb == 0, kb == nkb - 1))
                last = [b_R2] if kb >= nkb - 2 else []
                S.op("pe", mm_group(items), reads=[b_ckvS, b_ones] + rd + last, writes=PBo + PBd)
            rden, b_rden = FB, b_FB
            S.op("dve", I("tensor_scalar", rden[:], PSd, 1e-30, None, ALU.max), reads=PBd, writes=[b_rden])
            S.op("dve", I("reciprocal", rden[:], rden[:]), reads=[b_rden], writes=[b_rden])
            S.op("dve", I("tensor_tensor", OTn[:, hg * 1024:(hg + 1) * 1024], PSo, rden[:], ALU.mult),
                 reads=PBo + [b_rden], writes=[b_OTn])
        alias_open()
        ps_ob, pb_ob = pp()
        for hp in range(8):
            S.op("pe", mm_group([(ps_ob[:, hp * 128:(hp + 1) * 128], wuvb[:, (2 * hp) * 128:(2 * hp + 1) * 128], OTn[:, (2 * hp) * 128:(2 * hp + 1) * 128], True, False),
                                 (ps_ob[:, hp * 128:(hp + 1) * 128], wuvb[:, (2 * hp + 1) * 128:(2 * hp + 2) * 128], OTn[:, (2 * hp + 1) * 128:(2 * hp + 2) * 128], False, True)]),
                 reads=[b_wuv, b_OTn], writes=[pb_ob[hp // 4]])
        o_bT, b_obT = B2, b_B2
        S.op("act", I("copy", o_bT[:], ps_ob), reads=pb_ob, writes=[b_obT])

        o_aT, b_oaT = B3, b_B3
        ps_ya, pb_ya = proj_fm("wgp", o_aT, b_oaT)
        t1, b_t1 = FB, b_FB
        S.op("dve", I("tensor_tensor", t1[:], ps_ya, sgT[:, 0:1024], ALU.mult), reads=pb_ya + [b_sgT], writes=[b_t1])
        ps_yb, pb_yb = proj_fm("wdp", o_bT, b_obT)
        t2, b_t2 = FC, b_FC
        S.op("dve", I("tensor_tensor", t2[:], ps_yb, sgT[:, 1024:2048], ALU.mult), reads=pb_yb + [b_sgT], writes=[b_t2])
        mgT, b_mgT = B0, b_B0
        S.op("dve", I("tensor_tensor", mgT[:], t1[:], t2[:], ALU.add), reads=[b_t1, b_t2], writes=[b_mgT])

        ps_mx, pb_mx = proj_tm("wout", mgT, b_mgT)
        S.op("dve", I("scalar_tensor_tensor", FA[:], FA[:], ALPHA, ps_mx, ALU.mult, ALU.add), reads=[b_FA] + pb_mx, writes=[b_FA])
        x1, b_x1 = FE, b_FE
        layer_norm(FA[:], b_FA, x1[:], b_x1, V_L1G, V_L1B)
        x1b, b_x1b = B1, b_B1
        S.op("pool", I("tensor_copy", x1b[:], x1[:]), reads=[b_x1], writes=[b_x1b])
        ps_t, pb_t = ph()
        pstb = ps_t.bitcast(BF16)

        def ftx(e, pstb=pstb, x1b=x1b):
            ins = None
            for k in range(KD):
                ins = e.transpose(pstb[:, k * 128:(k + 1) * 128], x1b[:, k * 128:(k + 1) * 128], identb[:])
            return ins
        S.op("pe", ftx, reads=[b_x1b, b_idb], writes=pb_t)
        x1v = x1Te[:].rearrange("p (k t) -> p k t", k=KD)
        S.op("act", I("copy", x1v[:, :, 2:130], pstb.rearrange("p (k t) -> p k t", k=KD)), reads=pb_t, writes=[b_x1Te])
        if f == 0 or True:
            dbg("x1_%d" % f, x1[:], [b_x1])
        if halo:
            m = grp // 2
            S.op("dve", I("tensor_scalar", x1v[:, :, 0:2], x1v[:, :, 128:130], hflt[:, m:m + 1], None, ALU.mult),
                 reads=[b_x1Te, b_hfl], writes=[b_x1Te])
            return


    def gen_D(f, oi):
        x1, b_x1 = FE, b_FE
        x1v = x1Te[:].rearrange("p (k t) -> p k t", k=KD)
        actT, b_actT = OTn, b_OTn
        for sc_ in range(11):
            wt, wb_ = wstream("wup", 0, KD, sc_ * 512, 512, base_only=True)
            res = []
            for r in range(4):
                cc = sc_ * 4 + r
                ps_h, pb_h = ph()
                S.op("pe", mm_group([(ps_h[:, 0:130], wt[:, kc * 512 + r * 128: kc * 512 + (r + 1) * 128], x1Te[:, kc * 130:(kc + 1) * 130],
                                      kc == 0, kc == KD - 1) for kc in range(KD)]), reads=[b_x1Te, wb_], writes=pb_h)
                a, b_a = fa[r]
                S.op("act", I("activation", a[:], ps_h[:, 2:130], AF.Identity, bias=convt[:, cc * 4 + 3:cc * 4 + 4],
                                                                       scale=convt[:, cc * 4 + 2:cc * 4 + 3]), reads=pb_h + [b_conv], writes=[b_a])
                S.op("dve", I("scalar_tensor_tensor", a[:], ps_h[:, 1:129], convt[:, cc * 4 + 1:cc * 4 + 2], a[:], ALU.mult, ALU.add),
                     reads=pb_h + [b_conv, b_a], writes=[b_a])
                S.op("dve", I("scalar_tensor_tensor", a[:], ps_h[:, 0:128], convt[:, cc * 4:cc * 4 + 1], a[:], ALU.mult, ALU.add),
                     reads=pb_h + [b_conv, b_a], writes=[b_a])
                res.append((a, b_a))
            for r in range(2):
                ga_, b_ga = res[r]
                va_, b_va = res[2 + r]
                q = sc_ * 2 + r
                S.op("act", I("activation", ga_[:], ga_[:], AF.Silu), reads=[b_ga], writes=[b_ga])
                S.op("pool", I("tensor_tensor", actT[:, q * 128:(q + 1) * 128], ga_[:], va_[:], ALU.mult),
                     reads=[b_ga, b_va], writes=[b_actT])
            yield
        S.op("dve", I("tensor_copy", x1v[:, :, 0:2], x1v[:, :, 128:130]), reads=[b_x1Te], writes=[b_x1Te])
        ps_ff, pb_ff = pp()
        for hv in range(2):
            groups = [(0, 8), (8, 8), (16, 6)]
            for gi, (k0, nkc) in enumerate(groups):
                wt, wb_ = wstream("wdn", k0, nkc, hv * 512, 512, base_only=True)
                S.op("pe", mm_group([(ps_ff[:, hv * 512:(hv + 1) * 512], actT[:, (k0 + kc) * 128:(k0 + kc + 1) * 128], wt[:, kc * 512:(kc + 1) * 512],
                                      (gi == 0 and kc == 0), (gi == 2 and kc == nkc - 1)) for kc in range(nkc)]),
                     reads=[b_actT, wb_], writes=[pb_ff[hv]])
                yield
        x2p, b_x2p = FD, b_FD
        S.op("dve", I("scalar_tensor_tensor", x2p[:], x1[:], ALPHA, ps_ff, ALU.mult, ALU.add), reads=[b_x1] + pb_ff, writes=[b_x2p])
        x2, b_x2 = FC, b_FC
        layer_norm(x2p[:], b_x2p, x2[:], b_x2, V_L2G, V_L2B)
        yield
        x2b, b_x2b = B0, b_B0
        S.op("pool", I("tensor_copy", x2b[:], x2[:]), reads=[b_x2], writes=[b_x2b])
        x2T, b_x2T = B1, b_B1
        transposes8(x2b, b_x2b, x2T[:], b_x2T)
        yield
        ps_pg, pb_pg = proj_tm("wpg", x2T, b_x2T, base_only=True)
        sg2, b_sg2 = FB, b_FB
        S.op("act", I("activation", sg2[:], ps_pg, AF.Sigmoid), reads=pb_pg, writes=[b_sg2])
        yield
        p0 = oi * 128
        S.dma("pool", I("dma_start", out=pbt[:], in_=pown[p0:p0 + 128, :]), writes=[b_pbt])
        transposes8(pbt, b_pbt, pTt[:], b_pTt, nch=2)
        ps_pw, pb_pw = proj_tm("wple", pTt, b_pTt, nkc=2, base_only=True)
        S.op("dve", I("tensor_tensor", sg2[:], sg2[:], ps_pw, ALU.mult), reads=[b_sg2] + pb_pw, writes=[b_sg2])
        x3p, b_x3p = FD, b_FD
        S.op("dve", I("scalar_tensor_tensor", x3p[:], x2[:], ALPHA, sg2[:], ALU.mult, ALU.add), reads=[b_x2, b_sg2], writes=[b_x3p])
        layer_norm(x3p[:], b_x3p, FD[:], b_FD, V_L3G, V_L3B)
        S.dma("sp", I("dma_start", out=yown[p0:p0 + 128, :], in_=FD[:]), reads=[b_FD], is_output=True)

        yield

    pendingD = [None]
    for f in range(NB):
        grp = f // G
        own = (grp % 2 == 1)
        halo = (not own) and (f % G == G - 1)
        full = own or halo
        oi = (grp // 2) * G + f % G
        xb, b_xb = xbs[0]
        r0 = f * 128
        S.dma("pool", I("dma_start", out=xb[:], in_=xfr[r0:r0 + 128, :]), writes=[b_xb])
        if full:
            S.dma("sp", I("dma_start", out=FA[:], in_=xfr[r0:r0 + 128, :]), writes=[b_FA])
        flush_precast(len(pending_precast) if f >= G - 2 else 24)
        xT, b_xT = B1, b_B1
        transposes8(xb, b_xb, xT[:], b_xT)

        def xTk(kc):
            return xT[:, kc * 128:(kc + 1) * 128]

        wt, wb_ = wstream("win", 0, KD, C_K, 512)
        ps_k, pb_k = ph()
        S.op("pe", mm_group([(ps_k, xTk(kc), wt[:, kc * 512:(kc + 1) * 512], kc == 0, kc == KD - 1) for kc in range(KD)]),
             reads=[b_xT, wb_], writes=pb_k)
        k_tm, b_ktm = R0, b_R0
        S.op("act", I("copy", k_tm[:], ps_k), reads=pb_k, writes=[b_ktm])
        if full:
            ps_kT, pb_kT = ph()
            for hh in range(4):
                S.op("pe", mm_group([(ps_kT[:, hh * 128:(hh + 1) * 128], wt[:, kc * 512 + hh * 128: kc * 512 + (hh + 1) * 128], xTk(kc),
                                      kc == 0, kc == KD - 1) for kc in range(KD)]), reads=[b_xT, wb_], writes=pb_kT)
            S.op("act", I("copy", Q2[:], ps_kT), reads=pb_kT, writes=[b_Q2])
        ps_v, pb_v = pp()
        for hv in range(2):
            wt, wb_ = wstream("win", 0, KD, C_V + hv * 512, 512)
            S.op("pe", mm_group([(ps_v[:, hv * 512:(hv + 1) * 512], xTk(kc), wt[:, kc * 512:(kc + 1) * 512], kc == 0, kc == KD - 1)
                                 for kc in range(KD)]), reads=[b_xT, wb_], writes=[pb_v[hv]])
        v, b_v = B2, b_B2
        S.op("act", I("copy", v[:], ps_v), reads=pb_v, writes=[b_v])
        wt, wb_ = wstream("win", 0, KD, C_SM, 280)
        ps_s, pb_s = ph()
        S.op("pe", mm_group([(ps_s[:, 0:128], xTk(kc), wt[:, kc * 280: kc * 280 + 128], kc == 0, kc == KD - 1) for kc in range(KD)]),
             reads=[b_xT, wb_], writes=pb_s)
        if full:
            S.op("pe", mm_group([(ps_s[:, 128:136], xTk(kc), wt[:, kc * 280 + 272: kc * 280 + 280], kc == 0, kc == KD - 1)
                                 for kc in range(KD)]), reads=[b_xT, wb_], writes=pb_s)
        ps_f, pb_f = ph()
        S.op("pe", mm_group([(ps_f[:, 0:128], wt[:, kc * 280 + 128: kc * 280 + 256], xTk(kc), kc == 0, kc == KD - 1) for kc in range(KD)]),
             reads=[b_xT, wb_], writes=pb_f)
        S.op("pe", mm_group([(ps_f[0:16, 128:256], wt[:, kc * 280 + 256: kc * 280 + 272], xTk(kc), kc == 0, kc == KD - 1)
                             for kc in range(KD)]), reads=[b_xT, wb_], writes=pb_f)
        S.op("act", I("copy", ikT2[:, r0:r0 + 128], ps_f[:, 0:128]), reads=pb_f, writes=[b_ikT2])
        S.op("act", I("copy", gaT[0:16, :], ps_f[0:16, 128:256]), reads=pb_f, writes=[b_gaT])
        snk128 = sinkA[:, 0:1].broadcast_to([128, 128])
        S.op("act", I("activation", snk128, ps_s[:, 0:128], AF.Square, accum_out=sm[:, 8:9]), reads=pb_s, writes=[b_sm])
        rstd_from_sumsq(8, 9, 1, 1.0 / 128)
        S.op("dve", I("scalar_tensor_tensor", ckvS[:, r0:r0 + 128], ps_s[:, 0:128], sm[:, 9:10], gckvbc[:], ALU.mult, ALU.mult),
             reads=pb_s + [b_sm, b_gckv], writes=[b_ckvS])
        if full:
            S.op("dve", I("tensor_scalar", sm[:, 16:24], ps_s[:, 128:136], (8.0 ** -0.5) * (64.0 ** -0.5), None, ALU.mult),
                 reads=pb_s, writes=[b_sm])
        ps_t, pb_t = ph()
        pstb = ps_t.bitcast(BF16)
        S.op("pe", I("transpose", pstb[:, 0:128], ckvS[:, r0:r0 + 128], identb[:]), reads=[b_ckvS, b_idb], writes=pb_t)
        S.op("act", I("copy", ckvT[:, r0:r0 + 128], pstb[:, 0:128]), reads=pb_t, writes=[b_ckvT])

        ps_z, pb_z = ph()
        S.op("pe", I("matmul", ps_z, gaT[:, :], wgub[:, :], start=True, stop=True), reads=[b_gaT, b_wgu], writes=pb_z)
        Lt, b_L = R1, b_R1
        S.op("act", I("activation", Lt[:], ps_z, AF.Exp, scale=-1.0), reads=pb_z, writes=[b_L])
        S.op("act", I("activation", Lt[:], Lt[:], AF.Ln, bias=1.0), reads=[b_L], writes=[b_L])
        ps_e, pb_e = ph()
        S.op("pe", I("matmul", ps_e, C(O_UN), Lt[:], start=True, stop=True), reads=[b_cst, b_L], writes=pb_e)
        ps_d, pb_d = ph()
        for hh in range(4):
            S.op("pe", I("matmul", ps_d[:, hh:hh + 1], Lt[:, hh * 128:(hh + 1) * 128], cstt[:, O_TRIN + 127:O_TRIN + 128],
                                                 start=True, stop=True), reads=[b_cst, b_L], writes=pb_d)
        S.op("act", I("activation", sm[:, 24:28], ps_d[:, 0:4], AF.Exp), reads=pb_d, writes=[b_sm])
        est, b_est = R2, b_R2
        S.op("act", I("activation", est[:], ps_e, AF.Exp), reads=pb_e, writes=[b_est])
        kst, b_kst = Q0, b_Q0
        S.op("dve", I("tensor_tensor", kst[:], k_tm[:], est[:], ALU.mult), reads=[b_ktm, b_est], writes=[b_kst])

        if full:
            wt, wb_ = wstream("win", 0, KD, C_Q, 512)
            ps_qT, pb_qT = ph()
            for hh in range(4):
                S.op("pe", mm_group([(ps_qT[:, hh * 128:(hh + 1) * 128], wt[:, kc * 512 + hh * 128: kc * 512 + (hh + 1) * 128], xTk(kc),
                                      kc == 0, kc == KD - 1) for kc in range(KD)]), reads=[b_xT, wb_], writes=pb_qT)
            ps_b, pb_b = ph()
            for hh in range(4):
                S.op("pe", I("matmul", ps_b[:, hh * 128:(hh + 1) * 128], Lt[:, hh * 128:(hh + 1) * 128], C(O_TRIN),
                                                     start=True, stop=True), reads=[b_cst, b_L], writes=pb_b)
            S.op("act", I("activation", est[:], ps_b, AF.Exp), reads=pb_b, writes=[b_est])
            qin, b_qin = Q1, b_Q1
            S.op("dve", I("scalar_tensor_tensor", qin[:], ps_qT, 128.0 ** -0.5, est[:], ALU.mult, ALU.mult),
                 reads=pb_qT + [b_est], writes=[b_qin])
            S.op("act", I("activation", est[:], ps_b, AF.Exp, scale=-1.0), reads=pb_b + [b_qin], writes=[b_est])
            kin, b_kin = Q2, b_Q2
            S.op("dve", I("tensor_tensor", kin[:], kin[:], est[:], ALU.mult), reads=[b_kin, b_est], writes=[b_kin])
            ps_g, pb_g = pp()
            for hv in range(2):
                wt, wb_ = wstream("win", 0, KD, C_GR + hv * 512, 512)
                S.op("pe", mm_group([(ps_g[:, hv * 512:(hv + 1) * 512], xTk(kc), wt[:, kc * 512:(kc + 1) * 512], kc == 0, kc == KD - 1)
                                     for kc in range(KD)]), reads=[b_xT, wb_], writes=[pb_g[hv]])
            sgr, b_sgr = FB, b_FB
            S.op("act", I("activation", sgr[:], ps_g, AF.Silu), reads=pb_g, writes=[b_sgr])
            for ch in range(5):
                wt, wb_ = wstream("win", 0, KD, (C_DQ + ch * 512) if ch < 4 else C_IQ, 512)
                ps_q, pb_q = ph()
                for m in range(4):
                    S.op("pe", mm_group([(ps_q[:, m * 128:(m + 1) * 128], wt[:, kc * 512 + m * 128: kc * 512 + (m + 1) * 128], xTk(kc),
                                          kc == 0, kc == KD - 1) for kc in range(KD)]), reads=[b_xT, wb_], writes=pb_q)
                S.op("act", I("copy", Z[:, ch * 512:(ch + 1) * 512], ps_q), reads=pb_q, writes=[b_Z])
            for ch in range(4):
                wt, wb_ = wstream("win", 0, KD, C_GT + ch * 512, 512)
                ps_q, pb_q = ph()
                for m in range(4):
                    S.op("pe", mm_group([(ps_q[:, m * 128:(m + 1) * 128], wt[:, kc * 512 + m * 128: kc * 512 + (m + 1) * 128], xTk(kc),
                                          kc == 0, kc == KD - 1) for kc in range(KD)]), reads=[b_xT, wb_], writes=pb_q)
                S.op("act", I("activation", sgT[:, ch * 512:(ch + 1) * 512], ps_q, AF.Sigmoid),
                     reads=pb_q, writes=[b_sgT])

            ps_a, pb_a = ph()
            for hh in range(4):
                S.op("pe", I("matmul", ps_a[:, hh * 128:(hh + 1) * 128], kin[:, hh * 128:(hh + 1) * 128],
                                                     qin[:, hh * 128:(hh + 1) * 128], start=True, stop=True),
                     reads=[b_kin, b_qin], writes=pb_a)
            attT, b_att = Q3, b_Q3
            S.op("dve", I("tensor_tensor", attT[:].rearrange("p (h t) -> p h t", h=4), ps_a.rearrange("p (h t) -> p h t", h=4),
                                                  C(O_TRI01).unsqueeze(1).broadcast_to([128, 4, 128]), ALU.mult),
                 reads=pb_a + [b_cst], writes=[b_att])
            ps_o, pb_o = pp()
            for hh in range(4):
                S.op("pe", mm_group([(ps_o[:, hh * 256:(hh + 1) * 256], attT[:, hh * 128:(hh + 1) * 128], v[:, hh * 256:(hh + 1) * 256], True, False),
                                     (ps_o[:, hh * 256:(hh + 1) * 256], qin[:, hh * 128:(hh + 1) * 128], Sbf[:, hh * 256:(hh + 1) * 256], False, True)]),
                     reads=[b_att, b_v, b_qin, b_Sbf], writes=[pb_o[hh // 2]])
            snk256 = sinkA[:, 0:1].broadcast_to([128, 256])
            for hh in range(4):
                S.op("act", I("activation", snk256, ps_o[:, hh * 256:(hh + 1) * 256], AF.Square, accum_out=sm[:, 28 + hh:29 + hh]),
                     reads=[pb_o[hh // 2]], writes=[b_sm])
            rstd_from_sumsq(28, 32, 4, 1.0 / 256)
            o_n, b_on = FC, b_FC
            for hh in range(4):
                S.op("dve", I("scalar_tensor_tensor", o_n[:, hh * 256:(hh + 1) * 256], ps_o[:, hh * 256:(hh + 1) * 256],
                                                                    sm[:, 32 + hh:33 + hh], vecbc[:, V_GN * D + hh * 256: V_GN * D + (hh + 1) * 256],
                                                                    ALU.mult, ALU.mult),
                     reads=[pb_o[hh // 2], b_sm, b_vec], writes=[b_on])
            o_an, b_oan = B0, b_B0
            S.op("dve", I("tensor_tensor", o_an[:], o_n[:], sgr[:], ALU.mult), reads=[b_on, b_sgr], writes=[b_oan])
            o_aT, b_oaT = B3, b_B3
            transposes8(o_an, b_oan, o_aT[:], b_oaT)

        ps_kv, pb_kv = pp()
        for hh in range(4):
            S.op("pe", I("matmul", ps_kv[:, hh * 256:(hh + 1) * 256], kst[:, hh * 128:(hh + 1) * 128], v[:, hh * 256:(hh + 1) * 256],
                                                 start=True, stop=True), reads=[b_kst, b_v], writes=[pb_kv[hh // 2]])
        for hh in range(4):
            S.op("dve", I("scalar_tensor_tensor", Sst[:, hh * 256:(hh + 1) * 256], Sst[:, hh * 256:(hh + 1) * 256],
                                                                sm[:, 24 + hh:25 + hh], ps_kv[:, hh * 256:(hh + 1) * 256], ALU.mult, ALU.add),
                 reads=[b_S, b_sm, pb_kv[hh // 2]], writes=[b_S])
        S.op("pool", I("tensor_copy", Sbf[:], Sst[:]), reads=[b_S], writes=[b_Sbf])

        if not full:
            continue
        dsa_scores_and_setup(f)
        gens = [gen_bis(f)]
        if pendingD[0] is not None:
            gens.append(pendingD[0])
            pendingD[0] = None
        interleave(gens)
        attention_merge_ln1(f, halo, grp)
        if own:
            pendingD[0] = gen_D(f, oi)
    if pendingD[0] is not None:
        for _ in pendingD[0]:
            pass

    S.finish()
    S.emit()
    S.close()
    return nc


SPLIT = (512, 512, 1024, 1024, 16, 2048, 128, 512, 64, 8, 2048)


def host_consts(NIT):
    tok = np.arange(128)
    cst = np.zeros((128, NCST), np.float32)
    cst[:, O_ID:O_ID + 128] = np.eye(128, dtype=np.float32)
    cst[:, O_TRIN:O_TRIN + 128] = np.where(tok[:, None] <= tok[None, :], -1.0 / 16, 0.0)
    cst[:, O_UN:O_UN + 128] = np.where(tok[:, None] > tok[None, :], -1.0 / 16, 0.0)
    cst[:, O_TRI01:O_TRI01 + 128] = np.where(tok[:, None] <= tok[None, :], 1.0, 0.0)
    cst[:, O_DB:O_DB + 128] = np.where(tok[None, :] <= tok[:, None], 0.0, -BIG)
    cst[:, O_DBP:O_DBP + 128] = np.where(tok[None, :] <= tok[:, None], 0.0, BIG)
    cst[:, O_P2:O_P2 + NIT + 1] = (2.0 ** -(np.arange(NIT + 1) + 1.0))[None, :]
    cst[0:64, O_SEL] = 1.0
    cst[64:128, O_SEL + 1] = 1.0
    return cst


def prep_weights(w):
    offs = np.cumsum((0,) + SPLIT)
    seg = {n: (int(offs[i]), int(offs[i + 1])) for i, n in enumerate(
        ["gq", "gk", "gv", "gr", "ga", "dq", "ckv", "iq", "ik", "iw", "gates"])}
    wi = w["w_in"]

    def cols(n):
        a, b = seg[n]
        return wi[:, a:b]
    win2 = np.concatenate([cols("gk"), cols("gv"), cols("ckv"), cols("ik"), cols("ik"), cols("ga"), cols("iw"),
                           cols("gq"), cols("gr"), cols("dq"), cols("iq"), cols("gates")], axis=1)
    assert win2.shape[1] == NC2
    wgu = np.concatenate([w["w_gla_gate_up"], w["b_gla_gate"][None, :]], axis=0)
    wup = w["w_up"]
    pieces = []
    chan = []
    for i in range(11):
        pieces.append(wup[:, 256 * i:256 * i + 256])
        pieces.append(wup[:, DFF + 256 * i:DFF + 256 * i + 256])
        chan += [256 * i, 256 * i + 128, DFF + 256 * i, DFF + 256 * i + 128]
    wup2 = np.concatenate(pieces, axis=1)
    convp = np.zeros((128, NCC * 4), np.float32)
    for cc, ch0 in enumerate(chan):
        convp[:, cc * 4 + 0] = w["conv_w"][0, ch0:ch0 + 128]
        convp[:, cc * 4 + 1] = w["conv_w"][1, ch0:ch0 + 128]
        convp[:, cc * 4 + 2] = w["conv_w"][2, ch0:ch0 + 128]
        convp[:, cc * 4 + 3] = w["conv_b"][ch0:ch0 + 128]
    wuvp = np.zeros((128, 16, 128), np.float32)
    for h in range(16):
        wuvp[:, h, (h % 2) * 64:(h % 2) * 64 + 64] = w["w_uv"][h]
    vecs = np.stack([w["g_gla_norm"], w["ln1_g"], w["ln1_b"], w["ln2_g"], w["ln2_b"], w["ln3_g"], w["ln3_b"]], axis=0)
    f32 = lambda a: np.ascontiguousarray(a, dtype=np.float32)
    return dict(win2=f32(win2), wgu=f32(wgu), wgp=f32(w["w_gla_proj"]), wdp=f32(w["w_dsa_proj"]), wout=f32(w["w_out"]),
                wup2=f32(wup2), wdn=f32(w["w_down"]), wpg=f32(w["w_ple_gate"]), wple=f32(w["w_ple"]),
                wuvp=f32(wuvp.reshape(128, 2048)), vecs=f32(vecs), gckv=f32(w["g_ckv_norm"][None, :]), convp=f32(convp))


def run_module(x, p, w, G, NIT=16, dbg_names=()):
    B, L, _ = x.shape
    NB = L // 128
    assert NB % (2 * G) == 0
    topk = min(256, L // 4)
    nc = build_program(L, G, topk, NIT=NIT, dbg_names=dbg_names)
    shared = prep_weights(w)
    shared["cst"] = host_consts(NIT)
    own_frames = [f for f in range(NB) if (f // G) % 2 == 1]
    in_maps = []
    for b in range(B):
        for h in range(2):
            m = dict(shared)
            shift = G * (1 - h)
            xf = np.zeros((L, D), np.float32)
            xf[shift * 128:] = x[b, :L - shift * 128]
            m["xfr"] = xf
            m["pown"] = np.ascontiguousarray(np.concatenate([p[b, (f - shift) * 128:(f - shift + 1) * 128] for f in own_frames], axis=0))
            kv = np.zeros((128, 520), np.float32)
            if h == 0:
                kv[:, 0:512] = -BIG
                kv[:, 512:520] = 3.0 * BIG
            m["kvb"] = kv
            hf = np.ones((128, 8), np.float32)
            if h == 0:
                hf[:, 0] = 0.0
            m["hflag"] = hf
            in_maps.append(m)
    res = run_bass_kernel_spmd(nc, in_maps, core_ids=list(range(2 * B)))
    out = np.empty((B, L, D), np.float32)
    for b in range(B):
        for h in range(2):
            shift = G * (1 - h)
            y = res.results[b * 2 + h]["yown"]
            for oi, f in enumerate(own_frames):
                g = f - shift
                out[b, g * 128:(g + 1) * 128] = y[oi * 128:(oi + 1) * 128]
    dbgs = [{k: v for k, v in r.items() if k.startswith("dbg_")} for r in res.results]
    return out, dbgs


def kernel(**inputs):
    x = np.asarray(inputs["x"], dtype=np.float32)
    p = np.asarray(inputs["p"], dtype=np.float32)[0]
    names = ["w_in", "w_gla_gate_up", "b_gla_gate", "g_gla_norm", "w_gla_proj", "g_ckv_norm", "w_uv", "w_dsa_proj", "w_out",
             "ln1_g", "ln1_b", "w_up", "conv_w", "conv_b", "w_down", "ln2_g", "ln2_b", "w_ple", "w_ple_gate", "ln3_g", "ln3_b"]
    w = {n: np.asarray(inputs[n], dtype=np.float32)[0] for n in names}
    out, _ = run_module(x, p, w, G=16)
    return out
```
